# Optimizing a Trainium2 kernel written in Bass

```python
import jax, jax.numpy as jnp
from jax import lax
import numpy as np

D_MODEL = 1024
BATCH = 16
SEQ = 2048
DEPTH = 1
DEC_BATCH = 8
DEC_SEQ = 64
PAST_LEN = 4096

CHUNK = 64
QBLK = 128
SB_HEADS = 8
SB_HEAD_DIM = 128
SB_WIDTH = SB_HEADS * SB_HEAD_DIM
POOL_WINDOWS = (2, 4, 8, 16)
POOL_GROUPS = len(POOL_WINDOWS)
POOL_WIDTH = D_MODEL
POOL_GROUP_DIM = POOL_WIDTH // POOL_GROUPS
POOL_BUF = max(POOL_WINDOWS) - 1
IN_COLS = 4 * SB_WIDTH + 2 * POOL_WIDTH + 2 * D_MODEL
EPS = 1e-6

kernel_name = "stickbreak_pool_gated_hybrid_step"


def rmsnorm(x, g):
    xf = x.astype(jnp.float32)
    y = xf * lax.rsqrt(jnp.mean(xf * xf, axis=-1, keepdims=True) + EPS)
    return (y * g.astype(jnp.float32)).astype(x.dtype)


def split_proj(h, w_in):
    z = h @ w_in
    offs = np.cumsum([SB_WIDTH] * 4 + [POOL_WIDTH] * 2).tolist()
    q, k, v, ga, u, gb, mg = jnp.split(z, offs, axis=-1)
    b, t = h.shape[0], h.shape[1]
    hs = (b, t, SB_HEADS, SB_HEAD_DIM)
    return q.reshape(hs), k.reshape(hs), v.reshape(hs), ga, u, gb, mg


def stick_breaking(q, k, v, q_start):
    b, tq = q.shape[0], q.shape[1]
    tk = k.shape[1]
    z = jnp.einsum('bqhd,bkhd->bhqk', q, k,
                   preferred_element_type=jnp.float32) * (SB_HEAD_DIM ** -0.5)
    qpos = q_start + jnp.arange(tq)
    kpos = jnp.arange(tk)
    mask = kpos[None, :] < qpos[:, None]
    log_beta = jax.nn.log_sigmoid(z)
    log_1m = jnp.where(mask, log_beta - z, 0.0)
    after = lax.cumsum(log_1m, axis=3, reverse=True) - log_1m
    a = jnp.where(mask, jnp.exp(log_beta + after), 0.0)
    out = jnp.einsum('bhqk,bkhd->bqhd', a, v.astype(jnp.float32))
    return out.astype(q.dtype).reshape(b, tq, SB_WIDTH)


def multi_scale_pool(u, buf, pos_start, w_pool, pool_scale):
    b, t = u.shape[0], u.shape[1]
    ext = jnp.concatenate([buf.astype(u.dtype), u], axis=1).astype(jnp.float32)
    cs = jnp.cumsum(ext, axis=1)
    cs = jnp.concatenate([jnp.zeros_like(cs[:, :1]), cs], axis=1)
    pos = pos_start + jnp.arange(t)
    uf = u.astype(jnp.float32)
    outs = []
    for g, w in enumerate(POOL_WINDOWS):
        sl = slice(g * POOL_GROUP_DIM, (g + 1) * POOL_GROUP_DIM)
        hi = cs[:, POOL_BUF + 1:POOL_BUF + 1 + t, sl]
        lo = cs[:, POOL_BUF + 1 - w:POOL_BUF + 1 - w + t, sl]
        cnt = jnp.minimum(pos + 1, w).astype(jnp.float32)[None, :, None]
        outs.append((hi - lo) / cnt - uf[..., sl])
    p = jnp.concatenate(outs, axis=-1).astype(u.dtype)
    p = jnp.einsum('btgc,gcd->btgd', p.reshape(b, t, POOL_GROUPS, POOL_GROUP_DIM), w_pool)
    return p.reshape(b, t, POOL_WIDTH) * pool_scale


def merge_out(x, att, ga, pb, gb, mg, w_br_a, w_br_b, w_out):
    y_a = (att * jax.nn.silu(ga)) @ w_br_a
    y_b = (pb * jax.nn.silu(gb)) @ w_br_b
    gate = jax.nn.sigmoid(mg)
    m = gate[..., :D_MODEL] * y_a + gate[..., D_MODEL:] * y_b
    return x + m @ w_out


def setup_inputs(seed: int = 0) -> dict:
    key = jax.random.key(seed)
    ks = jax.random.split(key, 13)
    f = jnp.float32
    nrm = lambda k, s, sc: jax.random.normal(k, s, f) * sc
    return {
        "x_prompt": nrm(ks[0], (BATCH, SEQ, D_MODEL), 1.0),
        "x_sample": nrm(ks[1], (DEC_BATCH, DEC_SEQ, D_MODEL), 1.0),
        "cache_k": nrm(ks[2], (DEPTH, DEC_BATCH, PAST_LEN, SB_HEADS, SB_HEAD_DIM), 1.0),
        "cache_v": nrm(ks[3], (DEPTH, DEC_BATCH, PAST_LEN, SB_HEADS, SB_HEAD_DIM), 1.0),
        "state_pool": nrm(ks[4], (DEPTH, DEC_BATCH, POOL_BUF, POOL_WIDTH), 1.0),
        "norm_g": 1.0 + nrm(ks[5], (DEPTH, D_MODEL), 0.05),
        "w_in": nrm(ks[6], (DEPTH, D_MODEL, IN_COLS), D_MODEL ** -0.5),
        "w_pool": nrm(ks[7], (DEPTH, POOL_GROUPS, POOL_GROUP_DIM, POOL_GROUP_DIM), POOL_GROUP_DIM ** -0.5),
        "pool_scale": 1.0 + nrm(ks[8], (DEPTH, POOL_WIDTH), 0.1),
        "w_br_a": nrm(ks[9], (DEPTH, SB_WIDTH, D_MODEL), SB_WIDTH ** -0.5),
        "w_br_b": nrm(ks[10], (DEPTH, POOL_WIDTH, D_MODEL), POOL_WIDTH ** -0.5),
        "w_out": nrm(ks[11], (DEPTH, D_MODEL, D_MODEL), D_MODEL ** -0.5),
        "final_g": 1.0 + nrm(ks[12], (D_MODEL,), 0.05),
    }


def reference(x_prompt, x_sample, cache_k, cache_v, state_pool, norm_g, w_in, w_pool,
              pool_scale, w_br_a, w_br_b, w_out, final_g):
    xp, xs = x_prompt, x_sample
    past_len = cache_k.shape[2]
    kp_l, vp_l, pp_l, ks_l, vs_l, ps_l = [], [], [], [], [], []
    for l in range(DEPTH):
        h = rmsnorm(xp, norm_g[l])
        q, k, v, ga, u, gb, mg = split_proj(h, w_in[l])
        n_blk = xp.shape[1] // QBLK
        att = jnp.concatenate(
            [stick_breaking(q[:, i * QBLK:(i + 1) * QBLK], k[:, :(i + 1) * QBLK],
                            v[:, :(i + 1) * QBLK], i * QBLK) for i in range(n_blk)], axis=1)
        zero_buf = jnp.zeros((xp.shape[0], POOL_BUF, POOL_WIDTH), u.dtype)
        pb = multi_scale_pool(u, zero_buf, 0, w_pool[l], pool_scale[l])
        xp = merge_out(xp, att, ga, pb, gb, mg, w_br_a[l], w_br_b[l], w_out[l])
        kp_l.append(k)
        vp_l.append(v)
        pp_l.append(u[:, -POOL_BUF:])

        h = rmsnorm(xs, norm_g[l])
        q, k, v, ga, u, gb, mg = split_proj(h, w_in[l])
        k_all = jnp.concatenate([cache_k[l].astype(k.dtype), k], axis=1)
        v_all = jnp.concatenate([cache_v[l].astype(v.dtype), v], axis=1)
        att = stick_breaking(q, k_all, v_all, past_len)
        buf = state_pool[l].astype(u.dtype)
        pb = multi_scale_pool(u, buf, past_len, w_pool[l], pool_scale[l])
        xs = merge_out(xs, att, ga, pb, gb, mg, w_br_a[l], w_br_b[l], w_out[l])
        ks_l.append(k)
        vs_l.append(v)
        ps_l.append(jnp.concatenate([buf, u], axis=1)[:, -POOL_BUF:])

    y_prompt = rmsnorm(xp, final_g)
    y_sample = rmsnorm(xs, final_g)
    return (y_prompt, y_sample, jnp.stack(kp_l), jnp.stack(vp_l), jnp.stack(pp_l),
            jnp.stack(ks_l), jnp.stack(vs_l), jnp.stack(ps_l))
```

```python
import contextlib
import numpy as np
import concourse.bass as bass
import concourse.mybir as mybir
from concourse.bass_utils import run_bass_kernel_spmd

F32 = mybir.dt.float32
BF16 = mybir.dt.bfloat16
I32 = mybir.dt.int32
AF = mybir.ActivationFunctionType
ALU = mybir.AluOpType

ENGS = ("pe", "act", "dve", "pool", "sp")
D = 1024
KC = 8
NH = 8
POOLBUF = 15
SCALE = 128.0 ** -0.5
EPS = 1e-6
MASKV = -240.0


class Op:
    __slots__ = ("eng", "fn", "deps", "dma", "sig", "has_dep", "name")

    def __init__(self, eng, fn, dma, name):
        self.eng = eng
        self.fn = fn
        self.deps = []
        self.dma = dma
        self.sig = None
        self.has_dep = False
        self.name = name


class Prog:
    def __init__(self, nc):
        self.nc = nc
        self.q = {e: [] for e in ENGS}
        self.lastw = {}
        self.readers = {}
        self.all_ops = []
        self.last_of = {}
        self.pending_barrier = {}

    def barrier(self):
        ops = list(self.last_of.values())
        for e in ENGS:
            self.pending_barrier[e] = ops

    def op(self, eng, fn, reads=(), writes=(), dma=None, name=""):
        o = Op(eng, fn, dma, name)
        deps = {}
        key = lambda p: p.dma if p.dma else p.eng
        me = dma if dma else eng
        for r in reads:
            w = self.lastw.get(r)
            if w is not None:
                deps[id(w)] = w
        for wr in writes:
            w = self.lastw.get(wr)
            if w is not None and not (dma and key(w) == me):
                deps[id(w)] = w
            for rd in self.readers.get(wr, ()):
                if not (dma and key(rd) == me):
                    deps[id(rd)] = rd
        pb = self.pending_barrier.pop(eng, None)
        if pb:
            for d in pb:
                if key(d) != me or d.dma:
                    deps[id(d)] = d
        for d in deps.values():
            if d.eng == "pe" and eng == "pe" and not d.dma and not dma:
                continue
            o.deps.append(d)
            d.has_dep = True
        for r in reads:
            self.readers.setdefault(r, []).append(o)
        for wr in writes:
            self.lastw[wr] = o
            self.readers[wr] = []
        self.q[eng].append(o)
        self.all_ops.append(o)
        self.last_of[me] = o
        return o

    def emit(self, final_wait_eng="sp"):
        nc = self.nc
        counts = {}
        for o in self.all_ops:
            k = o.dma if o.dma else o.eng
            if o.dma or o.has_dep:
                inc = 16 if o.dma else 1
                counts[k] = counts.get(k, 0) + inc
                o.sig = (k, counts[k], inc)
        with contextlib.ExitStack() as st:
            sems = {k: st.enter_context(nc.semaphore("s_" + str(k))) for k in counts}
            blk = st.enter_context(nc.Block())

            def run_engine(ename):
                def body(e):
                    waited = {}
                    for o in self.q[ename]:
                        need = {}
                        for d in o.deps:
                            k, v, _ = d.sig
                            if v > need.get(k, 0):
                                need[k] = v
                        for k, v in need.items():
                            if waited.get(k, 0) >= v:
                                continue
                            e.wait_ge(sems[k], v)
                            waited[k] = v
                        ins = o.fn(e)
                        if o.sig is not None:
                            ins.then_inc(sems[o.sig[0]], o.sig[2])
                    if ename == final_wait_eng:
                        for k, v in counts.items():
                            if waited.get(k, 0) < v:
                                e.wait_ge(sems[k], v)
                return body

            blk.tensor(run_engine("pe"))
            blk.scalar(run_engine("act"))
            blk.vector(run_engine("dve"))
            blk.gpsimd(run_engine("pool"))
            blk.sync(run_engine("sp"))
        return counts


def build_nc(NSEQ=2, T=2048, PAST=4096, DS=64, MTOK=1024, with_sample=True):
    nc = bass.Bass("TRN2", target_bir_lowering=False)
    IN = lambda n, s: nc.dram_tensor(n, s, F32, kind="ExternalInput").ap()
    OUT = lambda n, s: nc.dram_tensor(n, s, F32, kind="ExternalOutput").ap()
    xp = IN("xp", [NSEQ, T, D])
    xs = IN("xs", [DS, D])
    ck = IN("ck", [PAST, D])
    cv = IN("cv", [PAST, D])
    spool = IN("spool", [POOLBUF, D])
    norm_g = IN("norm_g", [D])
    w_in = IN("w_in", [D, 8 * D])
    w_pool = IN("w_pool", [4, 256, 256])
    pool_scale = IN("pool_scale", [D])
    w_br_a = IN("w_br_a", [D, D])
    w_br_b = IN("w_br_b", [D, D])
    w_out = IN("w_out", [D, D])
    final_g = IN("final_g", [D])
    y_p = OUT("y_p", [NSEQ, T, D])
    y_s = OUT("y_s", [DS, D])
    k_p = OUT("k_p", [NSEQ, T, D])
    v_p = OUT("v_p", [NSEQ, T, D])
    pool_p = OUT("pool_p", [NSEQ, POOLBUF, D])
    k_s = OUT("k_s", [DS, D])
    v_s = OUT("v_s", [DS, D])
    pool_s = OUT("pool_s", [POOLBUF, D])

    TT = T // 128
    NPB = PAST // 128
    TMAX = max(T, DS)
    TX = T + (DS if with_sample else 0)

    class Arena:
        def __init__(self, base):
            self.off = base
            self.n = 0

        def alloc(self, name, shape, dt):
            esz = {F32: 4, BF16: 2, I32: 4}[dt]
            nb = esz
            for s in shape[1:]:
                nb *= s
            self.off = (self.off + 63) // 64 * 64
            t = nc.alloc_sbuf_tensor_at(name, shape, dt, offset=self.off)
            self.off += nb
            return t

    ar = Arena(16512)
    ident_bf = ar.alloc("ident_bf", [128, 128], BF16)
    ident_f = ar.alloc("ident_f", [128, 128], F32)
    negtri = ar.alloc("negtri", [128, 128], BF16)
    negones = ar.alloc("negones", [128, 128], BF16)
    maskneg = ar.alloc("maskneg", [128, 128], BF16)
    g_bc = ar.alloc("g_bc", [128, D], F32)
    fg_bc = ar.alloc("fg_bc", [128, D], F32)
    pscale = ar.alloc("pscale", [128, 8], F32)
    wpool_sb = ar.alloc("wpool_sb", [128, 4, 2, 256], BF16)
    rc = ar.alloc("rc", [128, 4, 16], F32)
    iot = ar.alloc("iot", [128, 16], I32)
    iotf = ar.alloc("iotf", [128, 16], F32)
    W = [ar.alloc("W%d" % i, [128, KC, 4, 128], BF16) for i in range(3)]
    hT = ar.alloc("hT", [128, KC, TX], BF16)
    GAT = ar.alloc("GAT", [128, KC, TX], BF16)
    ssT = ar.alloc("ssT", [128, 64], F32)
    rsT = ar.alloc("rsT", [128, 64], F32)
    Uhist = ar.alloc("Uhist", [128, 8, 16], F32)
    Uhist_s = ar.alloc("Uhist_s", [128, 8, 16], F32)
    region = ar.off
    ra = Arena(region)
    xt = [ra.alloc("xt%d" % i, [128, D], F32) for i in range(2)]
    hn = [ra.alloc("hn%d" % i, [128, D], BF16) for i in range(2)]
    kvst = [ra.alloc("kvst%d" % i, [128, 2, 4, 128], F32) for i in range(2)]
    Eb = [ra.alloc("E%d" % i, [128, 512], F32) for i in range(2)]
    SPb = [ra.alloc("SP%d" % i, [128, 512], BF16) for i in range(3)]
    ATb = [ra.alloc("AT%d" % i, [128, 512], BF16) for i in range(3)]
    SPacc = [ra.alloc("SPacc%d" % i, [128, 512], BF16) for i in range(4)]
    SPtmp = [ra.alloc("SPtmp%d" % i, [128, 256], BF16) for i in range(2)]
    qT = [ra.alloc("qT%d" % i, [128, T], BF16) for i in range(2)]
    kT = [ra.alloc("kT%d" % i, [128, T], BF16) for i in range(2)]
    sga = [ra.alloc("sga%d" % i, [128, T], BF16) for i in range(2)]
    vtok = [ra.alloc("vtok%d" % i, [128, TT, 128], BF16) for i in range(2)]
    qTs = [ra.alloc("qTs%d" % i, [128, DS], BF16) for i in range(2)]
    kTs = [ra.alloc("kTs%d" % i, [128, DS], BF16) for i in range(2)]
    sgas = [ra.alloc("sgas%d" % i, [128, DS], BF16) for i in range(2)]
    vtoks = [ra.alloc("vtoks%d" % i, [128, 128], BF16) for i in range(2)]
    kTc = ra.alloc("kTc", [128, PAST], BF16)
    ktokc = ra.alloc("ktokc", [128, NPB, 128], BF16)
    vtokc = ra.alloc("vtokc", [128, NPB, 128], BF16)
    spt = ra.alloc("spt", [16, D], F32)
    ktokb = ra.alloc("ktokb", [128, TT, 128], BF16)
    rb = Arena(region)
    wout = rb.alloc("wout", [128, KC, D], BF16)
    GBT = rb.alloc("GBT", [128, KC, MTOK + DS], BF16)
    MT = rb.alloc("MT", [128, KC, MTOK + DS], BF16)
    Ub = [rb.alloc("U%d" % i, [128, 2, 528], F32) for i in range(2)]
    Sb = [rb.alloc("S%d" % i, [128, 2, 528], F32) for i in range(2)]
    Pbf = [rb.alloc("Pbf%d" % i, [128, 2, 512], BF16) for i in range(2)]
    sgb = [rb.alloc("sgb%d" % i, [128, 512], BF16) for i in range(4)]
    sa = [rb.alloc("sa%d" % i, [128, 512], F32) for i in range(2)]
    sb_ = [rb.alloc("sb%d" % i, [128, 512], F32) for i in range(2)]
    ob = [rb.alloc("o%d" % i, [128, D], F32) for i in range(3)]
    tfix = rb.alloc("tfix", [128, 2, 16], F32)
    poolout = rb.alloc("poolout", [16, D], F32)
    print("SBUF end offsets: persistent", region, "A", ra.off, "B", rb.off)
    assert max(ra.off, rb.off) <= 229376, (ra.off, rb.off)

    psF = [nc.alloc_psum_tensor("psF%d" % i, [128, 512], F32) for i in range(7)]
    psT = nc.alloc_psum_tensor("psT", [128, 1024], BF16)
    PB = [0, 1]
    ZB = [2, 3, 4, 5]
    OB = [6]
    ALLB = [0, 1, 2, 3, 4, 5, 6]

    pg = Prog(nc)
    rot = {}

    def nxt(name, n):
        v = rot.get(name, 0)
        rot[name] = v + 1
        return v % n

    def mm(out, lhsT, rhs, start, stop, reads, writes, skip=False):
        pg.op("pe", lambda e: e.matmul(out, lhsT=lhsT, rhs=rhs, start=start, stop=stop, skip_group_check=skip),
              reads=reads, writes=writes)

    def consts():
        P = "pool"
        pg.op(P, lambda e: e.memset(ident_bf[:], 1.0), writes=["ident_bf"])
        pg.op(P, lambda e: e.affine_select(out=ident_bf[:], in_=ident_bf[:], pattern=[[-1, 128]], compare_op=ALU.is_equal,
                                           fill=0.0, base=0, channel_multiplier=1), reads=["ident_bf"], writes=["ident_bf"])
        pg.op(P, lambda e: e.memset(ident_f[:], 1.0), writes=["ident_f"])
        pg.op(P, lambda e: e.affine_select(out=ident_f[:], in_=ident_f[:], pattern=[[-1, 128]], compare_op=ALU.is_equal,
                                           fill=0.0, base=0, channel_multiplier=1), reads=["ident_f"], writes=["ident_f"])
        pg.op(P, lambda e: e.memset(negtri[:], -1.0), writes=["negtri"])
        pg.op(P, lambda e: e.affine_select(out=negtri[:], in_=negtri[:], pattern=[[-1, 128]], compare_op=ALU.is_ge,
                                           fill=0.0, base=0, channel_multiplier=1), reads=["negtri"], writes=["negtri"])
        pg.op(P, lambda e: e.memset(negones[:], -1.0), writes=["negones"])
        pg.op(P, lambda e: e.memset(maskneg[:], 0.0), writes=["maskneg"])
        pg.op(P, lambda e: e.affine_select(out=maskneg[:], in_=maskneg[:], pattern=[[1, 128]], compare_op=ALU.is_gt,
                                           fill=MASKV, base=0, channel_multiplier=-1), reads=["maskneg"], writes=["maskneg"])
        pg.op(P, lambda e: e.memset(Uhist[:], 0.0), writes=[("Uhist", g, j) for g in range(4) for j in range(2)])
        pg.op(P, lambda e: e.iota(iot[:], pattern=[[1, 16]], base=1, channel_multiplier=0), writes=["iot"])
        pg.op(P, lambda e: e.tensor_copy(out=iotf[:], in_=iot[:]), reads=["iot"], writes=["iotf"])
        for g in range(4):
            w = float(2 ** (g + 1))
            pg.op("dve", lambda e, g=g, w=w: e.tensor_scalar(out=rc[:, g, :], in0=iotf[:], scalar1=w, scalar2=None, op0=ALU.min),
                  reads=["iotf"], writes=[("rc", g)])
            pg.op("dve", lambda e, g=g: e.reciprocal(out=rc[:, g, :], in_=rc[:, g, :]), reads=[("rc", g)], writes=[("rc", g)])
        pg.op("sp", lambda e: e.dma_start(out=g_bc[:], in_=norm_g.partition_broadcast(128)), writes=["g_bc"], dma="c_g")

    def consts_late():
        pg.op("sp", lambda e: e.dma_start(out=fg_bc[:], in_=final_g.partition_broadcast(128)), writes=["fg_bc"], dma="c_fg")
        psv = pool_scale.rearrange("(fc p o) -> fc p o", p=128, o=1)
        for fc in range(8):
            pg.op("sp", lambda e, fc=fc: e.dma_start(out=pscale[:, fc:fc + 1], in_=psv[fc]), writes=["pscale"], dma="c_ps")
        for g in range(4):
            pg.op("pool", lambda e, g=g: e.dma_start(out=wpool_sb[:, g, :, :], in_=w_pool[g].rearrange("(j p) d -> p j d", p=128)),
                  writes=["wpool"], dma="c_wp")
    w_in_v = w_in.rearrange("(kc p) c -> p kc c", p=128)
    wbra_v = w_br_a.rearrange("(kc p) c -> p kc c", p=128)
    wbrb_v = w_br_b.rearrange("(kc p) c -> p kc c", p=128)

    def head_cols(h):
        return [(w_in_v, 0 * D + h * 128), (w_in_v, 1 * D + h * 128), (w_in_v, 2 * D + h * 128), (w_in_v, 3 * D + h * 128)]

    def bunit_cols(g):
        return [(w_in_v, 4 * D + 256 * g), (w_in_v, 4 * D + 256 * g + 128), (w_in_v, 5 * D + 256 * g), (w_in_v, 5 * D + 256 * g + 128)]

    def munit_cols(j):
        return [(w_in_v, 6 * D + 128 * j), (w_in_v, 7 * D + 128 * j), (wbra_v, 128 * j), (wbrb_v, 128 * j)]

    unit_list = []

    def plan_units(T_):
        for h in range(NH):
            unit_list.append(head_cols(h))
        for mi in range(max(T_ // min(MTOK, T_), 1)):
            for g in range(4):
                unit_list.append(bunit_cols(g))
            for j in range(8):
                unit_list.append(munit_cols(j))

    for s_ in range(NSEQ):
        plan_units(T)
    unit_state = dict(issued=0, used=0)

    def issue_unit():
        k = unit_state["issued"]
        if k >= len(unit_list):
            return
        s = k % 3
        for j, (src, c0) in enumerate(unit_list[k]):
            pg.op("pool", lambda e, s=s, j=j, src=src, c0=c0: e.dma_start(out=W[s][:, :, j, :], in_=src[:, :, c0:c0 + 128]),
                  writes=[("W", s)], dma="w%d" % s)
        unit_state["issued"] = k + 1

    def load_unit(cols):
        k = unit_state["used"]
        assert [c0 for (_, c0) in unit_list[k]] == [c0 for (_, c0) in cols], (k, cols)
        while unit_state["issued"] <= min(k + 2, len(unit_list) - 1):
            issue_unit()
        unit_state["used"] = k + 1
        return k % 3

    stat_i = [0]

    def phase_n(xsrc, ntok, tt):
        xs_ = nxt("xt", 2)
        hs = nxt("hn", 2)
        si = stat_i[0] % 64
        stat_i[0] += 1
        pg.op("sp", lambda e: e.dma_start(out=xt[xs_][0:ntok, :], in_=xsrc), writes=[("xt", xs_)], dma="x%d" % xs_)
        pg.op("act", lambda e: e.activation(out=hn[hs][0:ntok, :], in_=xt[xs_][0:ntok, :], func=AF.Square,
                                            accum_out=ssT[0:ntok, si:si + 1]),
              reads=[("xt", xs_)], writes=[("hn", hs), ("ss", si)])
        pg.op("act", lambda e: e.activation(out=rsT[0:ntok, si:si + 1], in_=ssT[0:ntok, si:si + 1], func=AF.Ln,
                                            scale=1.0 / D, bias=EPS),
              reads=[("ss", si)], writes=[("rs", si)])
        pg.op("act", lambda e: e.activation(out=rsT[0:ntok, si:si + 1], in_=rsT[0:ntok, si:si + 1], func=AF.Exp, scale=-0.5),
              reads=[("rs", si)], writes=[("rs", si)])
        pg.op("dve", lambda e: e.scalar_tensor_tensor(out=hn[hs][0:ntok, :], in0=xt[xs_][0:ntok, :], scalar=rsT[0:ntok, si:si + 1],
                                                      in1=g_bc[0:ntok, :], op0=ALU.mult, op1=ALU.mult),
              reads=[("xt", xs_), ("rs", si), "g_bc"], writes=[("hn", hs)])
        psT3 = psT[:].rearrange("p (k t) -> p k t", k=KC)
        for kc in range(KC):
            pg.op("pe", lambda e, kc=kc: e.transpose(out=psT3[:, kc, 0:ntok], in_=hn[hs][0:ntok, kc * 128:(kc + 1) * 128],
                                                     identity=ident_bf[0:ntok, 0:ntok]),
                  reads=[("hn", hs), "ident_bf"], writes=["psT"])
        pg.op("dve", lambda e: e.tensor_copy(out=hT[:, :, tt * 128:tt * 128 + ntok], in_=psT3[:, :, 0:ntok]),
              reads=["psT"], writes=[("hT", tt)])

    def proj_fm(slot, j, src, src_res, t0, n, evac):
        b = PB[nxt("PB", len(PB))]
        for kc in range(KC):
            mm(psF[b][:, 0:n], W[slot][:, kc, j, :], src[:, kc, t0:t0 + n], kc == 0, kc == KC - 1,
               reads=[("W", slot)] + src_res, writes=[("ps", b)])
        evac(psF[b][:, 0:n], ("ps", b))

    def proj_fm_halves(slot_fn, j, src, src_res, t0, n, evac):
        stt = {}

        def h1():
            stt["b"] = PB[nxt("PB", len(PB))]
            stt["slot"] = slot_fn()
            b, slot = stt["b"], stt["slot"]
            for kc in range(KC // 2):
                mm(psF[b][:, 0:n], W[slot][:, kc, j, :], src[:, kc, t0:t0 + n], kc == 0, False,
                   reads=[("W", slot)] + src_res, writes=[("ps", b)])

        def h2():
            b, slot = stt["b"], stt["slot"]
            for kc in range(KC // 2, KC):
                mm(psF[b][:, 0:n], W[slot][:, kc, j, :], src[:, kc, t0:t0 + n], False, kc == KC - 1,
                   reads=[("W", slot)] + src_res, writes=[("ps", b)])
            evac(psF[b][:, 0:n], ("ps", b))
        return [h1, h2]

    def attention(chunks, gate_fn, extra=()):
        S = len(chunks)
        st = [dict() for _ in range(S)]
        pair_cur = [0]
        extra = list(extra)
        nextra = len(extra)
        done_extra = [0]

        def stageA1(i):
            c = chunks[i]
            z = ZB[nxt("ZB", 4)]
            st[i].update(z=z)
            K, n, a0, qn = c["K"], c["n"], c["a0"], c["qn"]
            for g, (kT_ap, kres, _, _) in enumerate(c["groups"]):
                mm(psF[z][0:K, a0 + g * qn:a0 + (g + 1) * qn], kT_ap, c["q"], g == 0, True,
                   reads=[kres] + c["qres"], writes=[("ps", z)], skip=g > 0)
            if c["diag"]:
                dn = c["dn"]
                mm(psF[z][0:K, a0:a0 + dn], ident_bf[0:K, 0:K], maskneg[0:K, 0:dn], False, True,
                   reads=["ident_bf", "maskneg"], writes=[("ps", z)], skip=True)

        def stageA2a(i):
            c = chunks[i]
            z = st[i]["z"]
            er = nxt("E", 2)
            st[i].update(er=er)
            K, n, a0, qn = c["K"], c["n"], c["a0"], c["qn"]
            zap = psF[z][0:K, a0:a0 + n]
            pg.op("act", lambda e: e.activation(out=Eb[er][0:K, a0:a0 + n], in_=zap, func=AF.Exp),
                  reads=[("ps", z)], writes=[("E", er)])

        def stageA2b(i):
            c = chunks[i]
            er = st[i]["er"]
            sr = nxt("SP", 3)
            st[i].update(sr=sr)
            K, n, a0, qn = c["K"], c["n"], c["a0"], c["qn"]
            pg.op("act", lambda e: e.activation(out=SPb[sr][0:K, a0:a0 + n], in_=Eb[er][0:K, a0:a0 + n], func=AF.Ln, bias=1.0),
                  reads=[("E", er)], writes=[("SP", sr)])

        def stageB1(i):
            c = chunks[i]
            z, sr = st[i]["z"], st[i]["sr"]
            K, n, a0, qn = c["K"], c["n"], c["a0"], c["qn"]
            G = len(c["groups"])
            zap = psF[z][0:K, a0:a0 + n]
            cur = st[i]["pair"] * 2 + c["acc"]
            oth = st[i]["pair"] * 2 + 1 - c["acc"]
            mm(zap, negtri[0:K, 0:K], SPb[sr][0:K, a0:a0 + n], False, True, reads=["negtri", ("SP", sr)],
               writes=[("ps", z)], skip=True)
            if c["carry"]:
                if G == 1 and c["diag"]:
                    dn_ = c["dn"]
                    mm(psF[z][0:K, a0 + dn_:a0 + n], negones[:, 0:K], SPacc[cur][:, a0 + dn_:a0 + n], False, True,
                       reads=["negones", ("SPacc", cur)], writes=[("ps", z)], skip=True)
                elif G == 1:
                    mm(zap, negones[:, 0:K], SPacc[cur][:, a0:a0 + qn], False, True,
                       reads=["negones", ("SPacc", cur)], writes=[("ps", z)], skip=True)
                else:
                    mm(zap.rearrange("p (g q) -> p g q", g=G), negones[:, 0:K],
                       SPacc[cur][:, a0:a0 + qn].unsqueeze(1).to_broadcast([128, G, qn]), False, True,
                       reads=["negones", ("SPacc", cur)], writes=[("ps", z)], skip=True)
            for g2 in range(G - 1):
                ng = G - 1 - g2
                mm(psF[z][0:K, a0 + (g2 + 1) * qn:a0 + G * qn].rearrange("p (g q) -> p g q", g=ng), negones[:, 0:K],
                   SPb[sr][:, a0 + g2 * qn:a0 + (g2 + 1) * qn].unsqueeze(1).to_broadcast([128, ng, qn]), False, True,
                   reads=["negones", ("SP", sr)], writes=[("ps", z)], skip=True)
            if not c["last"]:
                if G == 1:
                    pg.op("dve", lambda e: e.tensor_tensor(out=SPacc[oth][0:K, a0:a0 + n], in0=SPacc[cur][0:K, a0:a0 + n],
                                                            in1=SPb[sr][0:K, a0:a0 + n], op=ALU.add),
                          reads=[("SPacc", cur), ("SP", sr)], writes=[("SPacc", oth)])
                else:
                    assert G in (2, 4, 8) and a0 == 0
                    t0_ = nxt("SPtmp", 2)
                    w = G * qn // 2
                    pg.op("dve", lambda e, w=w: e.tensor_tensor(out=SPtmp[t0_][:, 0:w], in0=SPb[sr][:, 0:w], in1=SPb[sr][:, w:2 * w], op=ALU.add),
                          reads=[("SP", sr)], writes=[("SPtmp", t0_)])
                    while w > qn:
                        w //= 2
                        pg.op("dve", lambda e, w=w: e.tensor_tensor(out=SPtmp[t0_][:, 0:w], in0=SPtmp[t0_][:, 0:w], in1=SPtmp[t0_][:, w:2 * w], op=ALU.add),
                              reads=[("SPtmp", t0_)], writes=[("SPtmp", t0_)])
                    pg.op("dve", lambda e: e.tensor_tensor(out=SPacc[oth][:, 0:qn], in0=SPacc[cur][:, 0:qn],
                                                           in1=SPtmp[t0_][:, 0:qn], op=ALU.add),
                          reads=[("SPacc", cur), ("SPtmp", t0_)], writes=[("SPacc", oth)])

        def stageB2(i):
            c = chunks[i]
            z = st[i]["z"]
            ar_ = nxt("AT", 3)
            st[i]["ar"] = ar_
            K, n, a0 = c["K"], c["n"], c["a0"]
            zap = psF[z][0:K, a0:a0 + n]
            pg.op("act", lambda e: e.activation(out=ATb[ar_][0:K, a0:a0 + n], in_=zap, func=AF.Exp),
                  reads=[("ps", z)], writes=[("AT", ar_)])

        def stageC(i):
            c = chunks[i]
            ar_ = st[i]["ar"]
            K, n, a0, qn = c["K"], c["n"], c["a0"], c["qn"]
            o = c["obank"]
            for g, (_, _, v_ap, vres) in enumerate(c["groups"]):
                fst = c["first"] and g == 0
                mm(psF[o][:, a0:a0 + qn], v_ap, ATb[ar_][0:K, a0 + g * qn:a0 + (g + 1) * qn], fst, True,
                   reads=[vres, ("AT", ar_)], writes=[("ps", o)], skip=not fst)
            if c["last"]:
                gate_fn(c, o)

        def pre(i):
            c = chunks[i]
            if c["first"]:
                pair_cur[0] = nxt("SPpair", 2)
                for k in range(2):
                    kk = pair_cur[0] * 2 + k
                    pg.op("dve", lambda e, kk=kk: e.memset(SPacc[kk][:], 0.0), writes=[("SPacc", kk)])
            st[i]["pair"] = pair_cur[0]

        pre(0)
        stageA1(0)
        for step in range(S + 3):
            if step + 1 < S:
                pre(step + 1)
                stageA1(step + 1)
            if 0 <= step - 1 < S:
                stageA2b(step - 1)
            if step < S:
                stageA2a(step)
            if 0 <= step - 2 < S:
                stageB2(step - 2)
            if 0 <= step - 3 < S:
                stageC(step - 3)
            if 0 <= step - 1 < S:
                stageB1(step - 1)
            if nextra:
                want = min(nextra, ((step + 1) * nextra + max(S - 3, 1) - 1) // max(S - 3, 1))
                while done_extra[0] < want:
                    extra[done_extra[0]]()
                    done_extra[0] += 1
        while done_extra[0] < nextra:
            extra[done_extra[0]]()
            done_extra[0] += 1

    def stage1_prompt(s, with_s):
        hres = lambda t0, n: [("hT", t) for t in range(t0 // 128, (t0 + n + 127) // 128)]
        kview = k_p[s].rearrange("(tt p) (h d) -> p tt h d", p=128, d=128)
        vview = v_p[s].rearrange("(tt p) (h d) -> p tt h d", p=128, d=128)
        NQ = T // 512
        psT3 = psT[:].rearrange("p (k t) -> p k t", k=KC)
        if with_s:
            pg.op("sp", lambda e: e.dma_start(out=spt[0:POOLBUF, :], in_=spool), writes=["spt"], dma="spt")
            for half in range(2):
                b = PB[nxt("PB", len(PB))]
                for q in range(4):
                    fc = half * 4 + q
                    pg.op("pe", lambda e, b=b, q=q, fc=fc: e.transpose(out=psF[b][:, q * 16:q * 16 + 16],
                                                                      in_=spt[0:16, fc * 128:(fc + 1) * 128],
                                                                      identity=ident_f[0:16, 0:16]),
                          reads=["spt", "ident_f"], writes=[("ps", b)])
                pg.op("dve", lambda e, b=b, half=half: e.tensor_copy(out=Uhist_s[:, half * 4:(half + 1) * 4, 1:16],
                                                                     in_=psF[b][:, 0:64].rearrange("p (q t) -> p q t", q=4)[:, :, 0:POOLBUF]),
                      reads=[("ps", b)], writes=[("UhistS", 2 * half + a, j) for a in range(2) for j in range(2)])
            ckv = ck.rearrange("(tt p) (h d) -> p tt h d", p=128, d=128)
            cvv = cv.rearrange("(tt p) (h d) -> p tt h d", p=128, d=128)

        def proj_items(h, bf):
            items = []
            sl = {}

            def slot_():
                if "s" not in sl:
                    sl["s"] = load_unit(head_cols(h))
                return sl["s"]

            bytc = {tc: [] for tc in range(NQ)}
            for tc in range(NQ):
                t0 = tc * 512
                bytc[tc] += proj_fm_halves(slot_, 0, hT, hres(t0, 512), t0, 512,
                                           lambda bap, bres, t0=t0, tc=tc: pg.op("dve", lambda e: e.tensor_scalar(
                                               out=qT[bf][:, t0:t0 + 512], in0=bap, scalar1=SCALE, scalar2=None, op0=ALU.mult),
                                               reads=[bres], writes=[("qT", bf, tc)]))
                bytc[tc] += proj_fm_halves(slot_, 3, hT, hres(t0, 512), t0, 512,
                                           lambda bap, bres, t0=t0, tc=tc: pg.op("dve", lambda e: e.tensor_copy(out=sga[bf][:, t0:t0 + 512], in_=bap),
                                                                                reads=[bres], writes=[("sga", bf, tc)]))
            for tt in range(TT):
                def it(tt=tt):
                    slot = slot_()
                    b = PB[nxt("PB", len(PB))]
                    ks = (tt // 4) % 2
                    for kc in range(KC):
                        mm(psF[b][:, 0:256], hT[:, kc, tt * 128:(tt + 1) * 128], W[slot][:, kc, 1:3, :].rearrange("p j c -> p (j c)"),
                           kc == 0, kc == KC - 1, reads=[("W", slot), ("hT", tt)], writes=[("ps", b)])
                    pg.op("dve", lambda e: e.tensor_copy(out=kvst[ks][:, :, tt % 4, :],
                                                         in_=psF[b][:, 0:256].rearrange("p (j c) -> p j c", j=2)),
                          reads=[("ps", b)], writes=[("kvst", ks)])
                    pg.op("dve", lambda e: e.tensor_copy(out=vtok[bf][:, tt, :], in_=psF[b][:, 128:256]),
                          reads=[("ps", b)], writes=[("vtok", bf, tt)])
                    pg.op("dve", lambda e: e.tensor_copy(out=ktokb[:, tt, :], in_=psF[b][:, 0:128]),
                          reads=[("ps", b)], writes=[("ktokb", tt)])
                    if tt % 8 == 7:
                        g8 = tt // 8
                        for q in range(8):
                            pg.op("pe", lambda e, q=q: e.transpose(out=psT3[:, q, :], in_=ktokb[:, g8 * 8 + q, :], identity=ident_bf[:]),
                                  reads=[("ktokb", g8 * 8 + q), "ident_bf"], writes=["psT"])
                        pg.op("dve", lambda e: e.tensor_copy(out=kT[bf][:, g8 * 1024:(g8 + 1) * 1024], in_=psT[:]),
                              reads=["psT"], writes=[("kT", bf, g8 * 8 + i) for i in range(8)])
                    if tt % 4 == 3:
                        g4 = tt // 4
                        seng = "pool" if h == 0 else "sp"
                        stag = "p" if h == 0 else ""
                        pg.op(seng, lambda e: e.dma_start(out=kview[:, g4 * 4:(g4 + 1) * 4, h, :], in_=kvst[ks][:, 0, :, :]),
                              reads=[("kvst", ks)], dma="ko%s%d" % (stag, ks))
                        pg.op(seng, lambda e: e.dma_start(out=vview[:, g4 * 4:(g4 + 1) * 4, h, :], in_=kvst[ks][:, 1, :, :]),
                              reads=[("kvst", ks)], dma="vo%s%d" % (stag, ks))
                bytc[tt // 4].append(it)
            for tc in range(NQ):
                items += bytc[tc]
            sl["bytc"] = bytc
            if with_s:
                def it_sq():
                    proj_fm(slot_(), 0, hT, [("hT", TT)], T, DS,
                            lambda bap, bres: pg.op("dve", lambda e: e.tensor_scalar(out=qTs[bf][:, :], in0=bap, scalar1=SCALE, scalar2=None,
                                                                                    op0=ALU.mult), reads=[bres], writes=[("qTs", bf)]))
                    proj_fm(slot_(), 1, hT, [("hT", TT)], T, DS,
                            lambda bap, bres: pg.op("dve", lambda e: e.tensor_copy(out=kTs[bf][:, :], in_=bap), reads=[bres], writes=[("kTs", bf)]))
                    proj_fm(slot_(), 3, hT, [("hT", TT)], T, DS,
                            lambda bap, bres: pg.op("dve", lambda e: e.tensor_copy(out=sgas[bf][:, :], in_=bap), reads=[bres], writes=[("sgas", bf)]))
                items.append(it_sq)

                def it_skv():
                    slot = slot_()
                    b = PB[nxt("PB", len(PB))]
                    ks = 0
                    for kc in range(KC):
                        mm(psF[b][0:DS, 0:256], hT[:, kc, T:T + DS], W[slot][:, kc, 1:3, :].rearrange("p j c -> p (j c)"),
                           kc == 0, kc == KC - 1, reads=[("W", slot), ("hT", TT)], writes=[("ps", b)])
                    pg.op("dve", lambda e: e.tensor_copy(out=kvst[ks][0:DS, :, 0, :], in_=psF[b][0:DS, 0:256].rearrange("p (j c) -> p j c", j=2)),
                          reads=[("ps", b)], writes=[("kvst", ks)])
                    pg.op("pool", lambda e: e.tensor_copy(out=vtoks[bf][0:DS, :], in_=kvst[ks][0:DS, 1, 0, :]),
                          reads=[("kvst", ks)], writes=[("vtoks", bf)])
                    pg.op("sp", lambda e: e.dma_start(out=k_s[:, h * 128:(h + 1) * 128], in_=kvst[ks][0:DS, 0, 0, :]),
                          reads=[("kvst", ks)], dma="ko%d" % ks)
                    pg.op("sp", lambda e: e.dma_start(out=v_s[:, h * 128:(h + 1) * 128], in_=kvst[ks][0:DS, 1, 0, :]),
                          reads=[("kvst", ks)], dma="vo%d" % ks)
                items.append(it_skv)
            return items

        def cache_load(h):
            pg.op("pool", lambda e: e.dma_start(out=ktokc[:], in_=ckv[:, :, h, :]), writes=["ktokc"], dma="ckl")
            pg.op("pool", lambda e: e.dma_start(out=vtokc[:], in_=cvv[:, :, h, :]), writes=["vtokc"], dma="cvl")

        def cache_tr_items():
            items = []
            for g8 in range(NPB // 8):
                def it(g8=g8):
                    for q in range(8):
                        kb = g8 * 8 + q
                        pg.op("pe", lambda e, q=q, kb=kb: e.transpose(out=psT3[:, q, :], in_=ktokc[:, kb, :], identity=ident_bf[:]),
                              reads=["ktokc", "ident_bf"], writes=["psT"])
                    pg.op("dve", lambda e: e.tensor_copy(out=kTc[:, g8 * 1024:(g8 + 1) * 1024], in_=psT[:]),
                          reads=["psT"], writes=[("kTc", g8)])
                items.append(it)
            return items

        def silu_ops(bf):
            for tc in range(NQ):
                pg.op("act", lambda e, tc=tc: e.activation(out=sga[bf][:, tc * 512:(tc + 1) * 512], in_=sga[bf][:, tc * 512:(tc + 1) * 512],
                                                           func=AF.Silu), reads=[("sga", bf, tc)], writes=[("sga", bf, tc)])
            if with_s:
                pg.op("act", lambda e: e.activation(out=sgas[bf][:, :], in_=sgas[bf][:, :], func=AF.Silu),
                      reads=[("sgas", bf)], writes=[("sgas", bf)])

        if with_s:
            phase_n(xs[:, :], DS, TT)
        items0 = proj_items(0, 0)
        n_per_tc = (len(items0) - (2 if with_s else 0)) // NQ
        for tt in range(4):
            phase_n(xp[s, tt * 128:(tt + 1) * 128, :], 128, tt)
        for tc in range(NQ):
            its = items0[tc * n_per_tc:(tc + 1) * n_per_tc]
            if tc + 1 < NQ:
                for k in range(4):
                    tt = 4 * (tc + 1) + k
                    phase_n(xp[s, tt * 128:(tt + 1) * 128, :], 128, tt)
                    for it in its[k * len(its) // 4:(k + 1) * len(its) // 4]:
                        it()
            else:
                for it in its:
                    it()
        for it in items0[NQ * n_per_tc:]:
            it()
        silu_ops(0)
        GS = 8
        for h in range(NH):
            bf = h % 2
            nxt_items = proj_items(h + 1, 1 - bf) if h + 1 < NH else []
            chunks = []
            for c in range(NQ):
                ob_ = OB[nxt("OB", 1)]
                accsel = 0
                for kb in range(4 * c + 3, -1, -1):
                    a0 = max(kb - 4 * c, 0) * 128
                    n = 512 - a0
                    first = kb == 4 * c + 3
                    chunks.append(dict(groups=[(kT[bf][:, kb * 128:(kb + 1) * 128], ("kT", bf, kb), vtok[bf][:, kb, :], ("vtok", bf, kb))],
                                       K=128, q=qT[bf][:, c * 512 + a0:(c + 1) * 512], qres=[("qT", bf, c)], qn=n, n=n, a0=a0,
                                       diag=kb >= 4 * c, dn=128, first=first, last=kb == 0,
                                       obank=ob_, carry=not first, acc=accsel, c=c))
                    accsel = 1 - accsel
            pre = []
            if with_s:
                cache_load(h)
                pre = cache_tr_items()
                ob_ = OB[nxt("OB", 1)]
                chunks.append(dict(groups=[(kTs[bf][:, :], ("kTs", bf), vtoks[bf][0:DS, :], ("vtoks", bf))], K=DS, q=qTs[bf][:, :],
                                   qres=[("qTs", bf)], qn=DS, n=DS, a0=0, diag=True, dn=DS, first=True, last=False, obank=ob_,
                                   carry=False, acc=0, c=NQ))
                accsel = 1
                kb = NPB - 1
                while kb >= 0:
                    gsz = min(GS, kb + 1)
                    if gsz < GS:
                        gsz = 1
                    grp = [(kTc[:, (kb - g) * 128:(kb - g + 1) * 128], ("kTc", (kb - g) // 8), vtokc[:, kb - g, :], "vtokc") for g in range(gsz)]
                    chunks.append(dict(groups=grp, K=128, q=qTs[bf][:, :], qres=[("qTs", bf)], qn=DS, n=DS * gsz, a0=0, diag=False, dn=0,
                                       first=False, last=(kb - gsz) < 0, obank=ob_, carry=True, acc=accsel, c=NQ))
                    accsel = 1 - accsel
                    kb -= gsz

            def gate(cd, o, h=h, bf=bf):
                c = cd["c"]
                if c < NQ:
                    pg.op("dve", lambda e: e.tensor_tensor(out=GAT[:, h, c * 512:(c + 1) * 512], in0=psF[o][:, :],
                                                           in1=sga[bf][:, c * 512:(c + 1) * 512], op=ALU.mult),
                          reads=[("ps", o), ("sga", bf, c)], writes=[("GAT", h, c)])
                else:
                    pg.op("dve", lambda e: e.tensor_tensor(out=GAT[:, h, T:T + DS], in0=psF[o][:, 0:DS], in1=sgas[bf][:, :], op=ALU.mult),
                          reads=[("ps", o), ("sgas", bf)], writes=[("GAT", h, NQ)])
            attention(chunks, gate, extra=pre + nxt_items)
            if h + 1 < NH:
                silu_ops(1 - bf)

    def stage2(s, with_s):
        nmac = T // MTOK
        wov = w_out.rearrange("(fc p) c -> p fc c", p=128)

        def load_wout():
            for half in range(2):
                pg.op("pool", lambda e, half=half: e.dma_start(out=wout[:, :, half * 512:(half + 1) * 512],
                                                             in_=wov[:, :, half * 512:(half + 1) * 512]),
                      writes=["wout"], dma="c_wo")
        hres = lambda t0, n: [("hT", t) for t in range(t0 // 128, (t0 + n + 127) // 128)]
        for mi in range(nmac):
            chunk_list = [(mi * MTOK + ci * 512, ci * 512, 512, Uhist, "Uhist", (mi == 0 and ci == 0), False) for ci in range(MTOK // 512)]
            if with_s and mi == nmac - 1:
                chunk_list.append((T, MTOK, DS, Uhist_s, "UhistS", False, True))
            pend = [None]

            def zstage(p):
                g, l0, n, pb, sgs = p
                for d in range(2):
                    fo = 2 * g + d
                    b1 = ALLB[nxt("AB", 7)]
                    for j in range(2):
                        mm(psF[b1][:, 0:n], wpool_sb[:, g, j, d * 128:(d + 1) * 128], Pbf[pb][:, j, 0:n], j == 0, j == 1,
                           reads=["wpool", ("Pbf", pb)], writes=[("ps", b1)])
                    sg = sgs[d]
                    pg.op("dve", lambda e, b1=b1, sg=sg, fo=fo: e.scalar_tensor_tensor(
                        out=GBT[:, fo, l0:l0 + n], in0=psF[b1][:, 0:n], scalar=pscale[:, fo:fo + 1], in1=sgb[sg][:, 0:n],
                        op0=ALU.mult, op1=ALU.mult), reads=[("ps", b1), ("sgb", sg), "pscale"], writes=[("GBT", fo, l0)])

            for g in range(4):
                wwin = 2 ** (g + 1)
                slot = load_unit(bunit_cols(g))
                for (t0, l0, n, uh, utag, fixf, iss) in chunk_list:
                    ub = nxt("U", 2)
                    for j in range(2):
                        proj_fm(slot, j, hT, hres(t0, n), t0, n,
                                lambda bap, bres, ub=ub, j=j, n=n: pg.op("dve", lambda e: e.tensor_copy(out=Ub[ub][:, j, 16:16 + n], in_=bap),
                                                                         reads=[bres], writes=[("U", ub, j)]))
                    L = 16 + n
                    for j, eng in ((0, "pool"), (1, "dve")):
                        pg.op(eng, lambda e, ub=ub, g=g, j=j, uh=uh: e.tensor_copy(out=Ub[ub][:, j, 0:16], in_=uh[:, 2 * g + j, :]),
                              reads=[(utag, g, j)], writes=[("U", ub, j)])
                        pg.op(eng, lambda e, ub=ub, g=g, j=j, uh=uh, n=n: e.tensor_copy(out=uh[:, 2 * g + j, :], in_=Ub[ub][:, j, n:n + 16]),
                              reads=[("U", ub, j)], writes=[(utag, g, j)])
                    sgs = []
                    for d in range(2):
                        b2 = ALLB[nxt("AB", 7)]
                        for kc in range(KC):
                            mm(psF[b2][:, 0:n], W[slot][:, kc, 2 + d, :], hT[:, kc, t0:t0 + n], kc == 0, kc == KC - 1,
                               reads=[("W", slot)] + hres(t0, n), writes=[("ps", b2)])
                        sg = nxt("sgb", 4)
                        sgs.append(sg)
                        pg.op("act", lambda e, b2=b2, sg=sg, n=n: e.activation(out=sgb[sg][:, 0:n], in_=psF[b2][:, 0:n], func=AF.Silu),
                              reads=[("ps", b2)], writes=[("sgb", sg)])
                    fin = {}
                    for j, eng in ((0, "pool"), (1, "dve")):
                        src = Ub[ub]
                        srcres = ("U", ub, j)
                        sh = 1
                        lo = 1
                        for lvl in range(g + 1):
                            si_ = lvl % 2
                            pg.op(eng, lambda e, src=src, si_=si_, lo=lo, sh=sh, j=j, L=L: e.tensor_tensor(
                                out=Sb[si_][:, j, lo:L], in0=src[:, j, lo:L], in1=src[:, j, lo - sh:L - sh], op=ALU.add),
                                reads=[srcres], writes=[("S", si_, j)])
                            src = Sb[si_]
                            srcres = ("S", si_, j)
                            sh *= 2
                            lo = 2 * lo + 1
                        fin[j] = (src, srcres)
                    src = fin[0][0]
                    srcres_l = [fin[0][1], fin[1][1]]
                    pb = nxt("Pbf", 2)
                    pg.op("dve", lambda e, src=src, ub=ub, pb=pb, wwin=wwin, n=n: e.scalar_tensor_tensor(
                        out=Pbf[pb][:, :, 0:n], in0=src[:, :, 16:16 + n], scalar=1.0 / wwin, in1=Ub[ub][:, :, 16:16 + n],
                        op0=ALU.mult, op1=ALU.subtract), reads=srcres_l + [("U", ub, 0), ("U", ub, 1)], writes=[("Pbf", pb)])
                    if fixf:
                        for jj in range(2):
                            pg.op("pool", lambda e, src=src, g=g, jj=jj: e.tensor_tensor(
                                out=tfix[:, jj, 0:POOLBUF], in0=src[:, jj, 16:16 + POOLBUF],
                                in1=rc[:, g, 0:POOLBUF], op=ALU.mult),
                                reads=srcres_l + [("rc", g)], writes=["tfix"])
                        pg.op("pool", lambda e, ub=ub, pb=pb: e.tensor_tensor(
                            out=Pbf[pb][:, :, 0:POOLBUF], in0=tfix[:, :, 0:POOLBUF], in1=Ub[ub][:, :, 16:16 + POOLBUF], op=ALU.subtract),
                            reads=["tfix", ("U", ub, 0), ("U", ub, 1)], writes=[("Pbf", pb)])
                    if pend[0] is not None:
                        zstage(pend[0])
                    pend[0] = (g, l0, n, pb, sgs)
            zstage(pend[0])
            for j in range(8):
                slot = load_unit(munit_cols(j))
                if mi == 0 and j == 2:
                    load_wout()
                for (t0, l0, n, uh, utag, fixf, iss) in chunk_list:
                    bb = [ALLB[nxt("AB", 7)] for _ in range(4)]
                    gres = [("GAT", fc, t0 // 512) for fc in range(8)]
                    bres_ = [("GBT", fc, l0) for fc in range(8)]
                    for kc in range(KC):
                        mm(psF[bb[0]][:, 0:n], W[slot][:, kc, 0, :], hT[:, kc, t0:t0 + n], kc == 0, kc == KC - 1,
                           reads=[("W", slot)] + hres(t0, n), writes=[("ps", bb[0])])
                    for kc in range(KC):
                        mm(psF[bb[1]][:, 0:n], W[slot][:, kc, 1, :], hT[:, kc, t0:t0 + n], kc == 0, kc == KC - 1,
                           reads=[("W", slot)] + hres(t0, n), writes=[("ps", bb[1])])
                    for kc in range(KC):
                        mm(psF[bb[2]][:, 0:n], W[slot][:, kc, 2, :], GAT[:, kc, t0:t0 + n], kc == 0, kc == KC - 1,
                           reads=[("W", slot)] + gres, writes=[("ps", bb[2])])
                    for kc in range(KC):
                        mm(psF[bb[3]][:, 0:n], W[slot][:, kc, 3, :], GBT[:, kc, l0:l0 + n], kc == 0, kc == KC - 1,
                           reads=[("W", slot)] + bres_, writes=[("ps", bb[3])])
                    r = nxt("sab", 2)
                    pg.op("act", lambda e, r=r, b=bb[0], n=n: e.activation(out=sa[r][:, 0:n], in_=psF[b][:, 0:n], func=AF.Sigmoid),
                          reads=[("ps", bb[0])], writes=[("sa", r)])
                    pg.op("act", lambda e, r=r, b=bb[1], n=n: e.activation(out=sb_[r][:, 0:n], in_=psF[b][:, 0:n], func=AF.Sigmoid),
                          reads=[("ps", bb[1])], writes=[("sb", r)])
                    pg.op("dve", lambda e, r=r, b=bb[2], n=n: e.tensor_tensor(out=sa[r][:, 0:n], in0=psF[b][:, 0:n], in1=sa[r][:, 0:n], op=ALU.mult),
                          reads=[("ps", bb[2]), ("sa", r)], writes=[("sa", r)])
                    pg.op("dve", lambda e, r=r, b=bb[3], n=n: e.tensor_tensor(out=sb_[r][:, 0:n], in0=psF[b][:, 0:n], in1=sb_[r][:, 0:n], op=ALU.mult),
                          reads=[("ps", bb[3]), ("sb", r)], writes=[("sb", r)])
                    pg.op("pool", lambda e, r=r, j=j, l0=l0, n=n: e.tensor_tensor(out=MT[:, j, l0:l0 + n], in0=sa[r][:, 0:n], in1=sb_[r][:, 0:n], op=ALU.add),
                          reads=[("sa", r), ("sb", r)], writes=[("MT", j, l0)])
            tiles = []
            for (t0, l0, n, uh, utag, fixf, iss) in chunk_list:
                if iss:
                    tiles.append((xs[:, :], y_s[:, :], DS, l0, l0))
                else:
                    for ti in range(n // 128):
                        g0 = t0 + ti * 128
                        tiles.append((xp[s, g0:g0 + 128, :], y_p[s, g0:g0 + 128, :], 128, l0 + ti * 128, l0))
            obsel = {}

            def xload(k):
                src_, _, rows, _, _ = tiles[k]
                orr = nxt("o", 3)
                obsel[k] = orr
                pg.op("sp", lambda e, orr=orr, src_=src_, rows=rows: e.dma_start(out=ob[orr][0:rows, :], in_=src_),
                      writes=[("o", orr)], dma="x2%d" % orr)

            xload(0)
            for k in range(len(tiles)):
                if k + 1 < len(tiles):
                    xload(k + 1)
                src_, dst_, rows, lt, l0 = tiles[k]
                orr = obsel[k]
                si = stat_i[0] % 64
                stat_i[0] += 1
                mres = [("MT", fc, l0) for fc in range(8)]
                for half in range(2):
                    b = ALLB[nxt("AB", 7)]
                    for kc in range(KC):
                        mm(psF[b][0:rows, :], MT[:, kc, lt:lt + rows], wout[:, kc, half * 512:(half + 1) * 512], kc == 0, kc == KC - 1,
                           reads=["wout"] + mres, writes=[("ps", b)])
                    pg.op("dve", lambda e, b=b, half=half, orr=orr, rows=rows: e.tensor_tensor(
                        out=ob[orr][0:rows, half * 512:(half + 1) * 512], in0=psF[b][0:rows, :],
                        in1=ob[orr][0:rows, half * 512:(half + 1) * 512], op=ALU.add),
                        reads=[("ps", b), ("o", orr)], writes=[("o", orr)])
                pg.op("act", lambda e, orr=orr, si=si, rows=rows: e.activation(out=Pbf[0][0:rows, :, :].rearrange("p a b -> p (a b)"), in_=ob[orr][0:rows, :], func=AF.Square,
                                                                              accum_out=ssT[0:rows, si:si + 1]),
                      reads=[("o", orr)], writes=[("Pbf", 0), ("ss", si)])
                pg.op("act", lambda e, si=si, rows=rows: e.activation(out=rsT[0:rows, si:si + 1], in_=ssT[0:rows, si:si + 1], func=AF.Ln,
                                                                      scale=1.0 / D, bias=EPS), reads=[("ss", si)], writes=[("rs", si)])
                pg.op("act", lambda e, si=si, rows=rows: e.activation(out=rsT[0:rows, si:si + 1], in_=rsT[0:rows, si:si + 1], func=AF.Exp, scale=-0.5),
                      reads=[("rs", si)], writes=[("rs", si)])
                pg.op("dve", lambda e, orr=orr, si=si, rows=rows: e.scalar_tensor_tensor(
                    out=ob[orr][0:rows, :], in0=ob[orr][0:rows, :], scalar=rsT[0:rows, si:si + 1], in1=fg_bc[0:rows, :],
                    op0=ALU.mult, op1=ALU.mult), reads=[("o", orr), ("rs", si), "fg_bc"], writes=[("o", orr)])
                pg.op("sp", lambda e, orr=orr, dst_=dst_, rows=rows: e.dma_start(out=dst_, in_=ob[orr][0:rows, :]),
                      reads=[("o", orr)], dma="y%d" % orr)
        outs = [(Uhist, "Uhist", pool_p[s])]
        if with_s:
            outs.append((Uhist_s, "UhistS", pool_s))
        for (uh, utag, dst_) in outs:
            for half in range(2):
                b = ALLB[nxt("AB", 7)]
                for q in range(4):
                    fc = half * 4 + q
                    pg.op("pe", lambda e, b=b, q=q, fc=fc, uh=uh: e.transpose(out=psF[b][0:POOLBUF, q * 128:(q + 1) * 128], in_=uh[:, fc, 1:16],
                                                                             identity=ident_f[:]),
                          reads=[(utag, fc // 2, fc % 2), "ident_f"], writes=[("ps", b)])
                pg.op("dve", lambda e, b=b, half=half: e.tensor_copy(out=poolout[0:POOLBUF, half * 512:(half + 1) * 512], in_=psF[b][0:POOLBUF, :]),
                      reads=[("ps", b)], writes=["poolout"])
            pg.op("sp", lambda e, dst_=dst_: e.dma_start(out=dst_, in_=poolout[0:POOLBUF, :]), reads=["poolout"], dma="po")

    issue_unit()
    consts()
    pg.op("pool", lambda e: e.memset(Uhist_s[:], 0.0), writes=[("UhistS", g, j) for g in range(4) for j in range(2)])
    for s in range(NSEQ):
        ws = with_sample and s == NSEQ - 1
        pg.barrier()
        for g in range(4):
            pg.op("pool", lambda e, g=g: e.memset(Uhist[:, 2 * g:2 * g + 2, :], 0.0), writes=[("Uhist", g, 0), ("Uhist", g, 1)])
        stage1_prompt(s, ws)
        if s == 0:
            consts_late()
        pg.barrier()
        stage2(s, ws)
    pg.emit()
    return nc


_NC_CACHE = {}


def kernel(x_prompt, x_sample, cache_k, cache_v, state_pool, norm_g, w_in, w_pool, pool_scale, w_br_a, w_br_b, w_out, final_g):
    n = 8
    f = lambda a: np.ascontiguousarray(np.asarray(a, dtype=np.float32))
    x_prompt, x_sample, cache_k, cache_v, state_pool = map(f, (x_prompt, x_sample, cache_k, cache_v, state_pool))
    B, T, Dm = x_prompt.shape
    DB, DS, _ = x_sample.shape
    PAST = cache_k.shape[2]
    NSEQ = B // n
    key = (NSEQ, T, PAST, DS)
    if key not in _NC_CACHE:
        _NC_CACHE[key] = build_nc(NSEQ=NSEQ, T=T, PAST=PAST, DS=DS)
    nc = _NC_CACHE[key]
    shared = dict(norm_g=f(norm_g[0]), w_in=f(w_in[0]), w_pool=f(w_pool[0]), pool_scale=f(pool_scale[0]),
                  w_br_a=f(w_br_a[0]), w_br_b=f(w_br_b[0]), w_out=f(w_out[0]), final_g=f(final_g))
    in_maps = []
    for c in range(n):
        m = dict(shared)
        m["xp"] = x_prompt[c * NSEQ:(c + 1) * NSEQ]
        m["xs"] = x_sample[c]
        m["ck"] = cache_k[0, c].reshape(PAST, Dm)
        m["cv"] = cache_v[0, c].reshape(PAST, Dm)
        m["spool"] = state_pool[0, c]
        in_maps.append(m)
    res = run_bass_kernel_spmd(nc, in_maps, core_ids=list(range(n)))
    R = res.results
    cat = lambda k: np.concatenate([np.asarray(r[k]) for r in R], axis=0)
    stk = lambda k: np.stack([np.asarray(r[k]) for r in R], axis=0)
    y_prompt = cat("y_p")
    y_sample = stk("y_s")
    k_prompt = cat("k_p").reshape(1, B, T, NH, 128)
    v_prompt = cat("v_p").reshape(1, B, T, NH, 128)
    pool_prompt = cat("pool_p").reshape(1, B, POOLBUF, Dm)
    k_sample = stk("k_s").reshape(1, DB, DS, NH, 128)
    v_sample = stk("v_s").reshape(1, DB, DS, NH, 128)
    pool_sample = stk("pool_s").reshape(1, DB, POOLBUF, Dm)
    return (y_prompt.astype(np.float32), y_sample.astype(np.float32), k_prompt, v_prompt, pool_prompt, k_sample, v_sample, pool_sample)
```

```python
import contextlib
import numpy as np
import concourse.bass as bass
import concourse.mybir as mybir
from concourse.bass_utils import run_bass_kernel_spmd

F32 = mybir.dt.float32
BF16 = mybir.dt.bfloat16
I32 = mybir.dt.int32
AF = mybir.ActivationFunctionType
ALU = mybir.AluOpType

ENGS = ("pe", "act", "dve", "pool", "sp")
D = 1024
KC = 8
NH = 8
POOLBUF = 15
SCALE = 128.0 ** -0.5
EPS = 1e-6
MASKV = -240.0


class Op:
    __slots__ = ("eng", "fn", "deps", "dma", "sig", "has_dep", "name")

    def __init__(self, eng, fn, dma, name):
        self.eng = eng
        self.fn = fn
        self.deps = []
        self.dma = dma
        self.sig = None
        self.has_dep = False
        self.name = name


class Prog:
    def __init__(self, nc):
        self.nc = nc
        self.q = {e: [] for e in ENGS}
        self.lastw = {}
        self.readers = {}
        self.all_ops = []
        self.last_of = {}
        self.pending_barrier = {}

    def barrier(self):
        ops = list(self.last_of.values())
        for e in ENGS:
            self.pending_barrier[e] = ops

    def op(self, eng, fn, reads=(), writes=(), dma=None, name=""):
        o = Op(eng, fn, dma, name)
        deps = {}
        key = lambda p: p.dma if p.dma else p.eng
        me = dma if dma else eng
        for r in reads:
            w = self.lastw.get(r)
            if w is not None:
                deps[id(w)] = w
        for wr in writes:
            w = self.lastw.get(wr)
            if w is not None and not (dma and key(w) == me):
                deps[id(w)] = w
            for rd in self.readers.get(wr, ()):
                if not (dma and key(rd) == me):
                    deps[id(rd)] = rd
        pb = self.pending_barrier.pop(eng, None)
        if pb:
            for d in pb:
                if key(d) != me or d.dma:
                    deps[id(d)] = d
        for d in deps.values():
            if d.eng == "pe" and eng == "pe" and not d.dma and not dma:
                continue
            o.deps.append(d)
            d.has_dep = True
        for r in reads:
            self.readers.setdefault(r, []).append(o)
        for wr in writes:
            self.lastw[wr] = o
            self.readers[wr] = []
        self.q[eng].append(o)
        self.all_ops.append(o)
        self.last_of[me] = o
        return o

    def emit(self, final_wait_eng="sp"):
        nc = self.nc
        counts = {}
        for o in self.all_ops:
            k = o.dma if o.dma else o.eng
            if o.dma or o.has_dep:
                inc = 16 if o.dma else 1
                counts[k] = counts.get(k, 0) + inc
                o.sig = (k, counts[k], inc)
        with contextlib.ExitStack() as st:
            sems = {k: st.enter_context(nc.semaphore("s_" + str(k))) for k in counts}
            blk = st.enter_context(nc.Block())

            def run_engine(ename):
                def body(e):
                    waited = {}
                    for o in self.q[ename]:
                        need = {}
                        for d in o.deps:
                            k, v, _ = d.sig
                            if v > need.get(k, 0):
                                need[k] = v
                        for k, v in need.items():
                            if waited.get(k, 0) >= v:
                                continue
                            e.wait_ge(sems[k], v)
                            waited[k] = v
                        ins = o.fn(e)
                        if o.sig is not None:
                            ins.then_inc(sems[o.sig[0]], o.sig[2])
                    if ename == final_wait_eng:
                        for k, v in counts.items():
                            if waited.get(k, 0) < v:
                                e.wait_ge(sems[k], v)
                return body

            blk.tensor(run_engine("pe"))
            blk.scalar(run_engine("act"))
            blk.vector(run_engine("dve"))
            blk.gpsimd(run_engine("pool"))
            blk.sync(run_engine("sp"))
        return counts


def build_nc(NSEQ=2, T=2048, PAST=4096, DS=64, MTOK=1024, with_sample=True):
    nc = bass.Bass("TRN2", target_bir_lowering=False)
    IN = lambda n, s: nc.dram_tensor(n, s, F32, kind="ExternalInput").ap()
    OUT = lambda n, s: nc.dram_tensor(n, s, F32, kind="ExternalOutput").ap()
    xp = IN("xp", [NSEQ, T, D])
    xs = IN("xs", [DS, D])
    ck = IN("ck", [PAST, D])
    cv = IN("cv", [PAST, D])
    spool = IN("spool", [POOLBUF, D])
    norm_g = IN("norm_g", [D])
    w_in = IN("w_in", [D, 8 * D])
    w_pool = IN("w_pool", [4, 256, 256])
    pool_scale = IN("pool_scale", [D])
    w_br_a = IN("w_br_a", [D, D])
    w_br_b = IN("w_br_b", [D, D])
    w_out = IN("w_out", [D, D])
    final_g = IN("final_g", [D])
    y_p = OUT("y_p", [NSEQ, T, D])
    y_s = OUT("y_s", [DS, D])
    k_p = OUT("k_p", [NSEQ, T, D])
    v_p = OUT("v_p", [NSEQ, T, D])
    pool_p = OUT("pool_p", [NSEQ, POOLBUF, D])
    k_s = OUT("k_s", [DS, D])
    v_s = OUT("v_s", [DS, D])
    pool_s = OUT("pool_s", [POOLBUF, D])

    TT = T // 128
    NPB = PAST // 128
    TMAX = max(T, DS)
    TX = T + (DS if with_sample else 0)

    class Arena:
        def __init__(self, base):
            self.off = base
            self.n = 0

        def alloc(self, name, shape, dt):
            esz = {F32: 4, BF16: 2, I32: 4}[dt]
            nb = esz
            for s in shape[1:]:
                nb *= s
            self.off = (self.off + 63) // 64 * 64
            t = nc.alloc_sbuf_tensor_at(name, shape, dt, offset=self.off)
            self.off += nb
            return t

    ar = Arena(16512)
    ident_bf = ar.alloc("ident_bf", [128, 128], BF16)
    ident_f = ar.alloc("ident_f", [128, 128], F32)
    negtri = ar.alloc("negtri", [128, 128], BF16)
    negones = ar.alloc("negones", [128, 128], BF16)
    maskneg = ar.alloc("maskneg", [128, 128], BF16)
    g_bc = ar.alloc("g_bc", [128, D], F32)
    fg_bc = ar.alloc("fg_bc", [128, D], F32)
    pscale = ar.alloc("pscale", [128, 8], F32)
    wpool_sb = ar.alloc("wpool_sb", [128, 4, 2, 256], BF16)
    rc = ar.alloc("rc", [128, 4, 16], F32)
    iot = ar.alloc("iot", [128, 16], I32)
    iotf = ar.alloc("iotf", [128, 16], F32)
    W = [ar.alloc("W%d" % i, [128, KC, 4, 128], BF16) for i in range(3)]
    hT = ar.alloc("hT", [128, KC, TX], BF16)
    GAT = ar.alloc("GAT", [128, KC, TX], BF16)
    ssT = ar.alloc("ssT", [128, 64], F32)
    rsT = ar.alloc("rsT", [128, 64], F32)
    Uhist = ar.alloc("Uhist", [128, 8, 16], F32)
    Uhist_s = ar.alloc("Uhist_s", [128, 8, 16], F32)
    region = ar.off
    ra = Arena(region)
    xt = [ra.alloc("xt%d" % i, [128, D], F32) for i in range(2)]
    hn = [ra.alloc("hn%d" % i, [128, D], BF16) for i in range(2)]
    kvst = [ra.alloc("kvst%d" % i, [128, 2, 4, 128], F32) for i in range(2)]
    Eb = [ra.alloc("E%d" % i, [128, 512], F32) for i in range(2)]
    SPb = [ra.alloc("SP%d" % i, [128, 512], BF16) for i in range(3)]
    ATb = [ra.alloc("AT%d" % i, [128, 512], BF16) for i in range(3)]
    SPacc = [ra.alloc("SPacc%d" % i, [128, 512], BF16) for i in range(4)]
    SPtmp = [ra.alloc("SPtmp%d" % i, [128, 256], BF16) for i in range(2)]
    qT = [ra.alloc("qT%d" % i, [128, T], BF16) for i in range(2)]
    kT = [ra.alloc("kT%d" % i, [128, T], BF16) for i in range(2)]
    sga = [ra.alloc("sga%d" % i, [128, T], BF16) for i in range(2)]
    vtok = [ra.alloc("vtok%d" % i, [128, TT, 128], BF16) for i in range(2)]
    qTs = [ra.alloc("qTs%d" % i, [128, DS], BF16) for i in range(2)]
    kTs = [ra.alloc("kTs%d" % i, [128, DS], BF16) for i in range(2)]
    sgas = [ra.alloc("sgas%d" % i, [128, DS], BF16) for i in range(2)]
    vtoks = [ra.alloc("vtoks%d" % i, [128, 128], BF16) for i in range(2)]
    kTc = ra.alloc("kTc", [128, PAST], BF16)
    ktokc = ra.alloc("ktokc", [128, NPB, 128], BF16)
    vtokc = ra.alloc("vtokc", [128, NPB, 128], BF16)
    spt = ra.alloc("spt", [16, D], F32)
    ktokb = ra.alloc("ktokb", [128, TT, 128], BF16)
    rb = Arena(region)
    wout = rb.alloc("wout", [128, KC, D], BF16)
    GBT = rb.alloc("GBT", [128, KC, MTOK + DS], BF16)
    MT = rb.alloc("MT", [128, KC, MTOK + DS], BF16)
    Ub = [rb.alloc("U%d" % i, [128, 2, 528], F32) for i in range(2)]
    Sb = [rb.alloc("S%d" % i, [128, 2, 528], F32) for i in range(2)]
    Pbf = [rb.alloc("Pbf%d" % i, [128, 2, 512], BF16) for i in range(2)]
    sgb = [rb.alloc("sgb%d" % i, [128, 512], BF16) for i in range(4)]
    sa = [rb.alloc("sa%d" % i, [128, 512], F32) for i in range(2)]
    sb_ = [rb.alloc("sb%d" % i, [128, 512], F32) for i in range(2)]
    ob = [rb.alloc("o%d" % i, [128, D], F32) for i in range(3)]
    tfix = rb.alloc("tfix", [128, 2, 16], F32)
    poolout = rb.alloc("poolout", [16, D], F32)
    print("SBUF end offsets: persistent", region, "A", ra.off, "B", rb.off)
    assert max(ra.off, rb.off) <= 229376, (ra.off, rb.off)

    psF = [nc.alloc_psum_tensor("psF%d" % i, [128, 512], F32) for i in range(7)]
    psT = nc.alloc_psum_tensor("psT", [128, 1024], BF16)
    PB = [0, 1]
    ZB = [2, 3, 4, 5]
    OB = [6]
    ALLB = [0, 1, 2, 3, 4, 5, 6]

    pg = Prog(nc)
    rot = {}

    def nxt(name, n):
        v = rot.get(name, 0)
        rot[name] = v + 1
        return v % n

    def mm(out, lhsT, rhs, start, stop, reads, writes, skip=False):
        pg.op("pe", lambda e: e.matmul(out, lhsT=lhsT, rhs=rhs, start=start, stop=stop, skip_group_check=skip),
              reads=reads, writes=writes)

    def consts():
        P = "pool"
        pg.op(P, lambda e: e.memset(ident_bf[:], 1.0), writes=["ident_bf"])
        pg.op(P, lambda e: e.affine_select(out=ident_bf[:], in_=ident_bf[:], pattern=[[-1, 128]], compare_op=ALU.is_equal,
                                           fill=0.0, base=0, channel_multiplier=1), reads=["ident_bf"], writes=["ident_bf"])
        pg.op(P, lambda e: e.memset(ident_f[:], 1.0), writes=["ident_f"])
        pg.op(P, lambda e: e.affine_select(out=ident_f[:], in_=ident_f[:], pattern=[[-1, 128]], compare_op=ALU.is_equal,
                                           fill=0.0, base=0, channel_multiplier=1), reads=["ident_f"], writes=["ident_f"])
        pg.op(P, lambda e: e.memset(negtri[:], -1.0), writes=["negtri"])
        pg.op(P, lambda e: e.affine_select(out=negtri[:], in_=negtri[:], pattern=[[-1, 128]], compare_op=ALU.is_ge,
                                           fill=0.0, base=0, channel_multiplier=1), reads=["negtri"], writes=["negtri"])
        pg.op(P, lambda e: e.memset(negones[:], -1.0), writes=["negones"])
        pg.op(P, lambda e: e.memset(maskneg[:], 0.0), writes=["maskneg"])
        pg.op(P, lambda e: e.affine_select(out=maskneg[:], in_=maskneg[:], pattern=[[1, 128]], compare_op=ALU.is_gt,
                                           fill=MASKV, base=0, channel_multiplier=-1), reads=["maskneg"], writes=["maskneg"])
        pg.op(P, lambda e: e.memset(Uhist[:], 0.0), writes=[("Uhist", g, j) for g in range(4) for j in range(2)])
        pg.op(P, lambda e: e.iota(iot[:], pattern=[[1, 16]], base=1, channel_multiplier=0), writes=["iot"])
        pg.op(P, lambda e: e.tensor_copy(out=iotf[:], in_=iot[:]), reads=["iot"], writes=["iotf"])
        for g in range(4):
            w = float(2 ** (g + 1))
            pg.op("dve", lambda e, g=g, w=w: e.tensor_scalar(out=rc[:, g, :], in0=iotf[:], scalar1=w, scalar2=None, op0=ALU.min),
                  reads=["iotf"], writes=[("rc", g)])
            pg.op("dve", lambda e, g=g: e.reciprocal(out=rc[:, g, :], in_=rc[:, g, :]), reads=[("rc", g)], writes=[("rc", g)])
        pg.op("sp", lambda e: e.dma_start(out=g_bc[:], in_=norm_g.partition_broadcast(128)), writes=["g_bc"], dma="c_g")

    def consts_late():
        pg.op("sp", lambda e: e.dma_start(out=fg_bc[:], in_=final_g.partition_broadcast(128)), writes=["fg_bc"], dma="c_fg")
        psv = pool_scale.rearrange("(fc p o) -> fc p o", p=128, o=1)
        for fc in range(8):
            pg.op("sp", lambda e, fc=fc: e.dma_start(out=pscale[:, fc:fc + 1], in_=psv[fc]), writes=["pscale"], dma="c_ps")
        for g in range(4):
            pg.op("pool", lambda e, g=g: e.dma_start(out=wpool_sb[:, g, :, :], in_=w_pool[g].rearrange("(j p) d -> p j d", p=128)),
                  writes=["wpool"], dma="c_wp")
    w_in_v = w_in.rearrange("(kc p) c -> p kc c", p=128)
    wbra_v = w_br_a.rearrange("(kc p) c -> p kc c", p=128)
    wbrb_v = w_br_b.rearrange("(kc p) c -> p kc c", p=128)

    def head_cols(h):
        return [(w_in_v, 0 * D + h * 128), (w_in_v, 1 * D + h * 128), (w_in_v, 2 * D + h * 128), (w_in_v, 3 * D + h * 128)]

    def bunit_cols(g):
        return [(w_in_v, 4 * D + 256 * g), (w_in_v, 4 * D + 256 * g + 128), (w_in_v, 5 * D + 256 * g), (w_in_v, 5 * D + 256 * g + 128)]

    def munit_cols(j):
        return [(w_in_v, 6 * D + 128 * j), (w_in_v, 7 * D + 128 * j), (wbra_v, 128 * j), (wbrb_v, 128 * j)]

    unit_list = []

    def plan_units(T_):
        for h in range(NH):
            unit_list.append(head_cols(h))
        for mi in range(max(T_ // min(MTOK, T_), 1)):
            for g in range(4):
                unit_list.append(bunit_cols(g))
            for j in range(8):
                unit_list.append(munit_cols(j))

    for s_ in range(NSEQ):
        plan_units(T)
    unit_state = dict(issued=0, used=0)

    def issue_unit():
        k = unit_state["issued"]
        if k >= len(unit_list):
            return
        s = k % 3
        for j, (src, c0) in enumerate(unit_list[k]):
            pg.op("pool", lambda e, s=s, j=j, src=src, c0=c0: e.dma_start(out=W[s][:, :, j, :], in_=src[:, :, c0:c0 + 128]),
                  writes=[("W", s)], dma="w%d" % s)
        unit_state["issued"] = k + 1

    def load_unit(cols):
        k = unit_state["used"]
        assert [c0 for (_, c0) in unit_list[k]] == [c0 for (_, c0) in cols], (k, cols)
        while unit_state["issued"] <= min(k + 2, len(unit_list) - 1):
            issue_unit()
        unit_state["used"] = k + 1
        return k % 3

    stat_i = [0]

    def phase_n(xsrc, ntok, tt):
        xs_ = nxt("xt", 2)
        hs = nxt("hn", 2)
        si = stat_i[0] % 64
        stat_i[0] += 1
        pg.op("sp", lambda e: e.dma_start(out=xt[xs_][0:ntok, :], in_=xsrc), writes=[("xt", xs_)], dma="x%d" % xs_)
        pg.op("act", lambda e: e.activation(out=hn[hs][0:ntok, :], in_=xt[xs_][0:ntok, :], func=AF.Square,
                                            accum_out=ssT[0:ntok, si:si + 1]),
              reads=[("xt", xs_)], writes=[("hn", hs), ("ss", si)])
        pg.op("act", lambda e: e.activation(out=rsT[0:ntok, si:si + 1], in_=ssT[0:ntok, si:si + 1], func=AF.Ln,
                                            scale=1.0 / D, bias=EPS),
              reads=[("ss", si)], writes=[("rs", si)])
        pg.op("act", lambda e: e.activation(out=rsT[0:ntok, si:si + 1], in_=rsT[0:ntok, si:si + 1], func=AF.Exp, scale=-0.5),
              reads=[("rs", si)], writes=[("rs", si)])
        pg.op("dve", lambda e: e.scalar_tensor_tensor(out=hn[hs][0:ntok, :], in0=xt[xs_][0:ntok, :], scalar=rsT[0:ntok, si:si + 1],
                                                      in1=g_bc[0:ntok, :], op0=ALU.mult, op1=ALU.mult),
              reads=[("xt", xs_), ("rs", si), "g_bc"], writes=[("hn", hs)])
        psT3 = psT[:].rearrange("p (k t) -> p k t", k=KC)
        for kc in range(KC):
            pg.op("pe", lambda e, kc=kc: e.transpose(out=psT3[:, kc, 0:ntok], in_=hn[hs][0:ntok, kc * 128:(kc + 1) * 128],
                                                     identity=ident_bf[0:ntok, 0:ntok]),
                  reads=[("hn", hs), "ident_bf"], writes=["psT"])
        pg.op("dve", lambda e: e.tensor_copy(out=hT[:, :, tt * 128:tt * 128 + ntok], in_=psT3[:, :, 0:ntok]),
              reads=["psT"], writes=[("hT", tt)])

    def proj_fm(slot, j, src, src_res, t0, n, evac):
        b = PB[nxt("PB", len(PB))]
        for kc in range(KC):
            mm(psF[b][:, 0:n], W[slot][:, kc, j, :], src[:, kc, t0:t0 + n], kc == 0, kc == KC - 1,
               reads=[("W", slot)] + src_res, writes=[("ps", b)])
        evac(psF[b][:, 0:n], ("ps", b))

    def proj_fm_halves(slot_fn, j, src, src_res, t0, n, evac):
        stt = {}

        def h1():
            stt["b"] = PB[nxt("PB", len(PB))]
            stt["slot"] = slot_fn()
            b, slot = stt["b"], stt["slot"]
            for kc in range(KC // 2):
                mm(psF[b][:, 0:n], W[slot][:, kc, j, :], src[:, kc, t0:t0 + n], kc == 0, False,
                   reads=[("W", slot)] + src_res, writes=[("ps", b)])

        def h2():
            b, slot = stt["b"], stt["slot"]
            for kc in range(KC // 2, KC):
                mm(psF[b][:, 0:n], W[slot][:, kc, j, :], src[:, kc, t0:t0 + n], False, kc == KC - 1,
                   reads=[("W", slot)] + src_res, writes=[("ps", b)])
            evac(psF[b][:, 0:n], ("ps", b))
        return [h1, h2]

    def attention(chunks, gate_fn, extra=()):
        S = len(chunks)
        st = [dict() for _ in range(S)]
        pair_cur = [0]
        extra = list(extra)
        nextra = len(extra)
        done_extra = [0]

        def stageA1(i):
            c = chunks[i]
            z = ZB[nxt("ZB", 4)]
            st[i].update(z=z)
            K, n, a0, qn = c["K"], c["n"], c["a0"], c["qn"]
            for g, (kT_ap, kres, _, _) in enumerate(c["groups"]):
                mm(psF[z][0:K, a0 + g * qn:a0 + (g + 1) * qn], kT_ap, c["q"], g == 0, True,
                   reads=[kres] + c["qres"], writes=[("ps", z)], skip=g > 0)
            if c["diag"]:
                dn = c["dn"]
                mm(psF[z][0:K, a0:a0 + dn], ident_bf[0:K, 0:K], maskneg[0:K, 0:dn], False, True,
                   reads=["ident_bf", "maskneg"], writes=[("ps", z)], skip=True)

        def stageA2a(i):
            c = chunks[i]
            z = st[i]["z"]
            er = nxt("E", 2)
            st[i].update(er=er)
            K, n, a0, qn = c["K"], c["n"], c["a0"], c["qn"]
            zap = psF[z][0:K, a0:a0 + n]
            pg.op("act", lambda e: e.activation(out=Eb[er][0:K, a0:a0 + n], in_=zap, func=AF.Exp),
                  reads=[("ps", z)], writes=[("E", er)])

        def stageA2b(i):
            c = chunks[i]
            er = st[i]["er"]
            sr = nxt("SP", 3)
            st[i].update(sr=sr)
            K, n, a0, qn = c["K"], c["n"], c["a0"], c["qn"]
            pg.op("act", lambda e: e.activation(out=SPb[sr][0:K, a0:a0 + n], in_=Eb[er][0:K, a0:a0 + n], func=AF.Ln, bias=1.0),
                  reads=[("E", er)], writes=[("SP", sr)])

        def stageB1(i):
            c = chunks[i]
            z, sr = st[i]["z"], st[i]["sr"]
            K, n, a0, qn = c["K"], c["n"], c["a0"], c["qn"]
            G = len(c["groups"])
            zap = psF[z][0:K, a0:a0 + n]
            cur = st[i]["pair"] * 2 + c["acc"]
            oth = st[i]["pair"] * 2 + 1 - c["acc"]
            mm(zap, negtri[0:K, 0:K], SPb[sr][0:K, a0:a0 + n], False, True, reads=["negtri", ("SP", sr)],
               writes=[("ps", z)], skip=True)
            if c["carry"]:
                if G == 1 and c["diag"]:
                    dn_ = c["dn"]
                    mm(psF[z][0:K, a0 + dn_:a0 + n], negones[:, 0:K], SPacc[cur][:, a0 + dn_:a0 + n], False, True,
                       reads=["negones", ("SPacc", cur)], writes=[("ps", z)], skip=True)
                elif G == 1:
                    mm(zap, negones[:, 0:K], SPacc[cur][:, a0:a0 + qn], False, True,
                       reads=["negones", ("SPacc", cur)], writes=[("ps", z)], skip=True)
                else:
                    mm(zap.rearrange("p (g q) -> p g q", g=G), negones[:, 0:K],
                       SPacc[cur][:, a0:a0 + qn].unsqueeze(1).to_broadcast([128, G, qn]), False, True,
                       reads=["negones", ("SPacc", cur)], writes=[("ps", z)], skip=True)
            for g2 in range(G - 1):
                ng = G - 1 - g2
                mm(psF[z][0:K, a0 + (g2 + 1) * qn:a0 + G * qn].rearrange("p (g q) -> p g q", g=ng), negones[:, 0:K],
                   SPb[sr][:, a0 + g2 * qn:a0 + (g2 + 1) * qn].unsqueeze(1).to_broadcast([128, ng, qn]), False, True,
                   reads=["negones", ("SP", sr)], writes=[("ps", z)], skip=True)
            if not c["last"]:
                if G == 1:
                    pg.op("dve", lambda e: e.tensor_tensor(out=SPacc[oth][0:K, a0:a0 + n], in0=SPacc[cur][0:K, a0:a0 + n],
                                                            in1=SPb[sr][0:K, a0:a0 + n], op=ALU.add),
                          reads=[("SPacc", cur), ("SP", sr)], writes=[("SPacc", oth)])
                else:
                    assert G in (2, 4, 8) and a0 == 0
                    t0_ = nxt("SPtmp", 2)
                    w = G * qn // 2
                    pg.op("dve", lambda e, w=w: e.tensor_tensor(out=SPtmp[t0_][:, 0:w], in0=SPb[sr][:, 0:w], in1=SPb[sr][:, w:2 * w], op=ALU.add),
                          reads=[("SP", sr)], writes=[("SPtmp", t0_)])
                    while w > qn:
                        w //= 2
                        pg.op("dve", lambda e, w=w: e.tensor_tensor(out=SPtmp[t0_][:, 0:w], in0=SPtmp[t0_][:, 0:w], in1=SPtmp[t0_][:, w:2 * w], op=ALU.add),
                              reads=[("SPtmp", t0_)], writes=[("SPtmp", t0_)])
                    pg.op("dve", lambda e: e.tensor_tensor(out=SPacc[oth][:, 0:qn], in0=SPacc[cur][:, 0:qn],
                                                           in1=SPtmp[t0_][:, 0:qn], op=ALU.add),
                          reads=[("SPacc", cur), ("SPtmp", t0_)], writes=[("SPacc", oth)])

        def stageB2(i):
            c = chunks[i]
            z = st[i]["z"]
            ar_ = nxt("AT", 3)
            st[i]["ar"] = ar_
            K, n, a0 = c["K"], c["n"], c["a0"]
            zap = psF[z][0:K, a0:a0 + n]
            pg.op("act", lambda e: e.activation(out=ATb[ar_][0:K, a0:a0 + n], in_=zap, func=AF.Exp),
                  reads=[("ps", z)], writes=[("AT", ar_)])

        def stageC(i):
            c = chunks[i]
            ar_ = st[i]["ar"]
            K, n, a0, qn = c["K"], c["n"], c["a0"], c["qn"]
            o = c["obank"]
            for g, (_, _, v_ap, vres) in enumerate(c["groups"]):
                fst = c["first"] and g == 0
                mm(psF[o][:, a0:a0 + qn], v_ap, ATb[ar_][0:K, a0 + g * qn:a0 + (g + 1) * qn], fst, True,
                   reads=[vres, ("AT", ar_)], writes=[("ps", o)], skip=not fst)
            if c["last"]:
                gate_fn(c, o)

        def pre(i):
            c = chunks[i]
            if c["first"]:
                pair_cur[0] = nxt("SPpair", 2)
                for k in range(2):
                    kk = pair_cur[0] * 2 + k
                    pg.op("pool", lambda e, kk=kk: e.memset(SPacc[kk][:], 0.0), writes=[("SPacc", kk)])
            st[i]["pair"] = pair_cur[0]

        pre(0)
        stageA1(0)
        for step in range(S + 3):
            if step + 1 < S:
                pre(step + 1)
                stageA1(step + 1)
            if 0 <= step - 1 < S:
                stageA2b(step - 1)
            if step < S:
                stageA2a(step)
            if 0 <= step - 2 < S:
                stageB2(step - 2)
            if 0 <= step - 3 < S:
                stageC(step - 3)
            if 0 <= step - 1 < S:
                stageB1(step - 1)
            if nextra:
                want = min(nextra, ((step + 1) * nextra + max(S - 3, 1) - 1) // max(S - 3, 1))
                while done_extra[0] < want:
                    extra[done_extra[0]]()
                    done_extra[0] += 1
        while done_extra[0] < nextra:
            extra[done_extra[0]]()
            done_extra[0] += 1

    def stage1_prompt(s, with_s):
        hres = lambda t0, n: [("hT", t) for t in range(t0 // 128, (t0 + n + 127) // 128)]
        kview = k_p[s].rearrange("(tt p) (h d) -> p tt h d", p=128, d=128)
        vview = v_p[s].rearrange("(tt p) (h d) -> p tt h d", p=128, d=128)
        NQ = T // 512
        psT3 = psT[:].rearrange("p (k t) -> p k t", k=KC)
        if with_s:
            pg.op("sp", lambda e: e.dma_start(out=spt[0:POOLBUF, :], in_=spool), writes=["spt"], dma="spt")
            for half in range(2):
                b = PB[nxt("PB", len(PB))]
                for q in range(4):
                    fc = half * 4 + q
                    pg.op("pe", lambda e, b=b, q=q, fc=fc: e.transpose(out=psF[b][:, q * 16:q * 16 + 16],
                                                                      in_=spt[0:16, fc * 128:(fc + 1) * 128],
                                                                      identity=ident_f[0:16, 0:16]),
                          reads=["spt", "ident_f"], writes=[("ps", b)])
                pg.op("dve", lambda e, b=b, half=half: e.tensor_copy(out=Uhist_s[:, half * 4:(half + 1) * 4, 1:16],
                                                                     in_=psF[b][:, 0:64].rearrange("p (q t) -> p q t", q=4)[:, :, 0:POOLBUF]),
                      reads=[("ps", b)], writes=[("UhistS", 2 * half + a, j) for a in range(2) for j in range(2)])
            ckv = ck.rearrange("(tt p) (h d) -> p tt h d", p=128, d=128)
            cvv = cv.rearrange("(tt p) (h d) -> p tt h d", p=128, d=128)

        def proj_items(h, bf):
            items = []
            sl = {}

            def slot_():
                if "s" not in sl:
                    sl["s"] = load_unit(head_cols(h))
                return sl["s"]

            bytc = {tc: [] for tc in range(NQ)}
            for tc in range(NQ):
                t0 = tc * 512
                bytc[tc] += proj_fm_halves(slot_, 0, hT, hres(t0, 512), t0, 512,
                                           lambda bap, bres, t0=t0, tc=tc: pg.op("dve", lambda e: e.tensor_scalar(
                                               out=qT[bf][:, t0:t0 + 512], in0=bap, scalar1=SCALE, scalar2=None, op0=ALU.mult),
                                               reads=[bres], writes=[("qT", bf, tc)]))
                bytc[tc] += proj_fm_halves(slot_, 3, hT, hres(t0, 512), t0, 512,
                                           lambda bap, bres, t0=t0, tc=tc: pg.op("dve", lambda e: e.tensor_copy(out=sga[bf][:, t0:t0 + 512], in_=bap),
                                                                                reads=[bres], writes=[("sga", bf, tc)]))
            for tt in range(TT):
                def it(tt=tt):
                    slot = slot_()
                    b = PB[nxt("PB", len(PB))]
                    ks = (tt // 4) % 2
                    for kc in range(KC):
                        mm(psF[b][:, 0:256], hT[:, kc, tt * 128:(tt + 1) * 128], W[slot][:, kc, 1:3, :].rearrange("p j c -> p (j c)"),
                           kc == 0, kc == KC - 1, reads=[("W", slot), ("hT", tt)], writes=[("ps", b)])
                    pg.op("dve", lambda e: e.tensor_copy(out=kvst[ks][:, :, tt % 4, :],
                                                         in_=psF[b][:, 0:256].rearrange("p (j c) -> p j c", j=2)),
                          reads=[("ps", b)], writes=[("kvst", ks)])
                    pg.op("dve", lambda e: e.tensor_copy(out=vtok[bf][:, tt, :], in_=psF[b][:, 128:256]),
                          reads=[("ps", b)], writes=[("vtok", bf, tt)])
                    pg.op("dve", lambda e: e.tensor_copy(out=ktokb[:, tt, :], in_=psF[b][:, 0:128]),
                          reads=[("ps", b)], writes=[("ktokb", tt)])
                    if tt % 8 == 7:
                        g8 = tt // 8
                        for q in range(8):
                            pg.op("pe", lambda e, q=q: e.transpose(out=psT3[:, q, :], in_=ktokb[:, g8 * 8 + q, :], identity=ident_bf[:]),
                                  reads=[("ktokb", g8 * 8 + q), "ident_bf"], writes=["psT"])
                        pg.op("dve", lambda e: e.tensor_copy(out=kT[bf][:, g8 * 1024:(g8 + 1) * 1024], in_=psT[:]),
                              reads=["psT"], writes=[("kT", bf, g8 * 8 + i) for i in range(8)])
                    if tt % 4 == 3:
                        g4 = tt // 4
                        seng = "pool" if h == 0 else "sp"
                        stag = "p" if h == 0 else ""
                        pg.op(seng, lambda e: e.dma_start(out=kview[:, g4 * 4:(g4 + 1) * 4, h, :], in_=kvst[ks][:, 0, :, :]),
                              reads=[("kvst", ks)], dma="ko%s%d" % (stag, ks))
                        pg.op(seng, lambda e: e.dma_start(out=vview[:, g4 * 4:(g4 + 1) * 4, h, :], in_=kvst[ks][:, 1, :, :]),
                              reads=[("kvst", ks)], dma="vo%s%d" % (stag, ks))
                bytc[tt // 4].append(it)
            for tc in range(NQ):
                items += bytc[tc]
            sl["bytc"] = bytc
            if with_s:
                def it_sq():
                    proj_fm(slot_(), 0, hT, [("hT", TT)], T, DS,
                            lambda bap, bres: pg.op("dve", lambda e: e.tensor_scalar(out=qTs[bf][:, :], in0=bap, scalar1=SCALE, scalar2=None,
                                                                                    op0=ALU.mult), reads=[bres], writes=[("qTs", bf)]))
                    proj_fm(slot_(), 1, hT, [("hT", TT)], T, DS,
                            lambda bap, bres: pg.op("dve", lambda e: e.tensor_copy(out=kTs[bf][:, :], in_=bap), reads=[bres], writes=[("kTs", bf)]))
                    proj_fm(slot_(), 3, hT, [("hT", TT)], T, DS,
                            lambda bap, bres: pg.op("dve", lambda e: e.tensor_copy(out=sgas[bf][:, :], in_=bap), reads=[bres], writes=[("sgas", bf)]))
                items.append(it_sq)

                def it_skv():
                    slot = slot_()
                    b = PB[nxt("PB", len(PB))]
                    ks = 0
                    for kc in range(KC):
                        mm(psF[b][0:DS, 0:256], hT[:, kc, T:T + DS], W[slot][:, kc, 1:3, :].rearrange("p j c -> p (j c)"),
                           kc == 0, kc == KC - 1, reads=[("W", slot), ("hT", TT)], writes=[("ps", b)])
                    pg.op("dve", lambda e: e.tensor_copy(out=kvst[ks][0:DS, :, 0, :], in_=psF[b][0:DS, 0:256].rearrange("p (j c) -> p j c", j=2)),
                          reads=[("ps", b)], writes=[("kvst", ks)])
                    pg.op("pool", lambda e: e.tensor_copy(out=vtoks[bf][0:DS, :], in_=kvst[ks][0:DS, 1, 0, :]),
                          reads=[("kvst", ks)], writes=[("vtoks", bf)])
                    pg.op("sp", lambda e: e.dma_start(out=k_s[:, h * 128:(h + 1) * 128], in_=kvst[ks][0:DS, 0, 0, :]),
                          reads=[("kvst", ks)], dma="ko%d" % ks)
                    pg.op("sp", lambda e: e.dma_start(out=v_s[:, h * 128:(h + 1) * 128], in_=kvst[ks][0:DS, 1, 0, :]),
                          reads=[("kvst", ks)], dma="vo%d" % ks)
                items.append(it_skv)
            return items

        def cache_load_k(h):
            pg.op("pool", lambda e: e.dma_start(out=ktokc[:], in_=ckv[:, :, h, :]), writes=["ktokc"], dma="ckl")

        def cache_load_v(h):
            pg.op("pool", lambda e: e.dma_start(out=vtokc[:], in_=cvv[:, :, h, :]), writes=["vtokc"], dma="cvl")

        def cache_tr_items():
            items = []
            for g8 in range(NPB // 8):
                def it(g8=g8):
                    for q in range(8):
                        kb = g8 * 8 + q
                        pg.op("pe", lambda e, q=q, kb=kb: e.transpose(out=psT3[:, q, :], in_=ktokc[:, kb, :], identity=ident_bf[:]),
                              reads=["ktokc", "ident_bf"], writes=["psT"])
                    pg.op("dve", lambda e: e.tensor_copy(out=kTc[:, g8 * 1024:(g8 + 1) * 1024], in_=psT[:]),
                          reads=["psT"], writes=[("kTc", g8)])
                items.append(it)
            return items

        def silu_ops(bf):
            for tc in range(NQ):
                pg.op("act", lambda e, tc=tc: e.activation(out=sga[bf][:, tc * 512:(tc + 1) * 512], in_=sga[bf][:, tc * 512:(tc + 1) * 512],
                                                           func=AF.Silu), reads=[("sga", bf, tc)], writes=[("sga", bf, tc)])
            if with_s:
                pg.op("act", lambda e: e.activation(out=sgas[bf][:, :], in_=sgas[bf][:, :], func=AF.Silu),
                      reads=[("sgas", bf)], writes=[("sgas", bf)])

        if with_s:
            cache_load_k(0)
            phase_n(xs[:, :], DS, TT)
        items0 = proj_items(0, 0)
        n_per_tc = (len(items0) - (2 if with_s else 0)) // NQ
        for tt in range(4):
            phase_n(xp[s, tt * 128:(tt + 1) * 128, :], 128, tt)
        for tc in range(NQ):
            its = items0[tc * n_per_tc:(tc + 1) * n_per_tc]
            if tc + 1 < NQ:
                for k in range(4):
                    tt = 4 * (tc + 1) + k
                    phase_n(xp[s, tt * 128:(tt + 1) * 128, :], 128, tt)
                    for it in its[k * len(its) // 4:(k + 1) * len(its) // 4]:
                        it()
            else:
                for it in its:
                    it()
        for it in items0[NQ * n_per_tc:]:
            it()
        silu_ops(0)
        GS = 8
        for h in range(NH):
            bf = h % 2
            nxt_items = proj_items(h + 1, 1 - bf) if h + 1 < NH else []
            chunks = []
            for c in range(NQ):
                ob_ = OB[nxt("OB", 1)]
                accsel = 0
                for kb in range(4 * c + 3, -1, -1):
                    a0 = max(kb - 4 * c, 0) * 128
                    n = 512 - a0
                    first = kb == 4 * c + 3
                    chunks.append(dict(groups=[(kT[bf][:, kb * 128:(kb + 1) * 128], ("kT", bf, kb), vtok[bf][:, kb, :], ("vtok", bf, kb))],
                                       K=128, q=qT[bf][:, c * 512 + a0:(c + 1) * 512], qres=[("qT", bf, c)], qn=n, n=n, a0=a0,
                                       diag=kb >= 4 * c, dn=128, first=first, last=kb == 0,
                                       obank=ob_, carry=not first, acc=accsel, c=c))
                    accsel = 1 - accsel
            pre = []
            if with_s:
                cache_load_v(h)
                pre = cache_tr_items()
                if h + 1 < NH:
                    nxt_items = nxt_items[:6] + [lambda h=h: cache_load_k(h + 1)] + nxt_items[6:]
                ob_ = OB[nxt("OB", 1)]
                chunks.append(dict(groups=[(kTs[bf][:, :], ("kTs", bf), vtoks[bf][0:DS, :], ("vtoks", bf))], K=DS, q=qTs[bf][:, :],
                                   qres=[("qTs", bf)], qn=DS, n=DS, a0=0, diag=True, dn=DS, first=True, last=False, obank=ob_,
                                   carry=False, acc=0, c=NQ))
                accsel = 1
                kb = NPB - 1
                while kb >= 0:
                    gsz = min(GS, kb + 1)
                    if gsz < GS:
                        gsz = 1
                    grp = [(kTc[:, (kb - g) * 128:(kb - g + 1) * 128], ("kTc", (kb - g) // 8), vtokc[:, kb - g, :], "vtokc") for g in range(gsz)]
                    chunks.append(dict(groups=grp, K=128, q=qTs[bf][:, :], qres=[("qTs", bf)], qn=DS, n=DS * gsz, a0=0, diag=False, dn=0,
                                       first=False, last=(kb - gsz) < 0, obank=ob_, carry=True, acc=accsel, c=NQ))
                    accsel = 1 - accsel
                    kb -= gsz

            def gate(cd, o, h=h, bf=bf):
                c = cd["c"]
                if c < NQ:
                    pg.op("dve", lambda e: e.tensor_tensor(out=GAT[:, h, c * 512:(c + 1) * 512], in0=psF[o][:, :],
                                                           in1=sga[bf][:, c * 512:(c + 1) * 512], op=ALU.mult),
                          reads=[("ps", o), ("sga", bf, c)], writes=[("GAT", h, c)])
                else:
                    pg.op("dve", lambda e: e.tensor_tensor(out=GAT[:, h, T:T + DS], in0=psF[o][:, 0:DS], in1=sgas[bf][:, :], op=ALU.mult),
                          reads=[("ps", o), ("sgas", bf)], writes=[("GAT", h, NQ)])
            attention(chunks, gate, extra=pre + nxt_items)
            if h + 1 < NH:
                silu_ops(1 - bf)

    def stage2(s, with_s):
        nmac = T // MTOK
        wov = w_out.rearrange("(fc p) c -> p fc c", p=128)

        def load_wout():
            for half in range(2):
                pg.op("pool", lambda e, half=half: e.dma_start(out=wout[:, :, half * 512:(half + 1) * 512],
                                                             in_=wov[:, :, half * 512:(half + 1) * 512]),
                      writes=["wout"], dma="c_wo")
        hres = lambda t0, n: [("hT", t) for t in range(t0 // 128, (t0 + n + 127) // 128)]
        for mi in range(nmac):
            chunk_list = [(mi * MTOK + ci * 512, ci * 512, 512, Uhist, "Uhist", (mi == 0 and ci == 0), False) for ci in range(MTOK // 512)]
            if with_s and mi == nmac - 1:
                chunk_list.append((T, MTOK, DS, Uhist_s, "UhistS", False, True))
            pend = [None]

            def zstage(p):
                g, l0, n, pb, sgs = p
                for d in range(2):
                    fo = 2 * g + d
                    b1 = ALLB[nxt("AB", 7)]
                    for j in range(2):
                        mm(psF[b1][:, 0:n], wpool_sb[:, g, j, d * 128:(d + 1) * 128], Pbf[pb][:, j, 0:n], j == 0, j == 1,
                           reads=["wpool", ("Pbf", pb)], writes=[("ps", b1)])
                    sg = sgs[d]
                    pg.op("dve", lambda e, b1=b1, sg=sg, fo=fo: e.scalar_tensor_tensor(
                        out=GBT[:, fo, l0:l0 + n], in0=psF[b1][:, 0:n], scalar=pscale[:, fo:fo + 1], in1=sgb[sg][:, 0:n],
                        op0=ALU.mult, op1=ALU.mult), reads=[("ps", b1), ("sgb", sg), "pscale"], writes=[("GBT", fo, l0)])

            for g in range(4):
                wwin = 2 ** (g + 1)
                slot = load_unit(bunit_cols(g))
                for (t0, l0, n, uh, utag, fixf, iss) in chunk_list:
                    ub = nxt("U", 2)
                    for j in range(2):
                        proj_fm(slot, j, hT, hres(t0, n), t0, n,
                                lambda bap, bres, ub=ub, j=j, n=n: pg.op("dve", lambda e: e.tensor_copy(out=Ub[ub][:, j, 16:16 + n], in_=bap),
                                                                         reads=[bres], writes=[("U", ub, j)]))
                    L = 16 + n
                    for j, eng in ((0, "pool"), (1, "dve")):
                        pg.op(eng, lambda e, ub=ub, g=g, j=j, uh=uh: e.tensor_copy(out=Ub[ub][:, j, 0:16], in_=uh[:, 2 * g + j, :]),
                              reads=[(utag, g, j)], writes=[("U", ub, j)])
                        pg.op(eng, lambda e, ub=ub, g=g, j=j, uh=uh, n=n: e.tensor_copy(out=uh[:, 2 * g + j, :], in_=Ub[ub][:, j, n:n + 16]),
                              reads=[("U", ub, j)], writes=[(utag, g, j)])
                    sgs = []
                    for d in range(2):
                        b2 = ALLB[nxt("AB", 7)]
                        for kc in range(KC):
                            mm(psF[b2][:, 0:n], W[slot][:, kc, 2 + d, :], hT[:, kc, t0:t0 + n], kc == 0, kc == KC - 1,
                               reads=[("W", slot)] + hres(t0, n), writes=[("ps", b2)])
                        sg = nxt("sgb", 4)
                        sgs.append(sg)
                        pg.op("act", lambda e, b2=b2, sg=sg, n=n: e.activation(out=sgb[sg][:, 0:n], in_=psF[b2][:, 0:n], func=AF.Silu),
                              reads=[("ps", b2)], writes=[("sgb", sg)])
                    fin = {}
                    for j, eng in ((0, "pool"), (1, "dve")):
                        src = Ub[ub]
                        srcres = ("U", ub, j)
                        sh = 1
                        lo = 1
                        for lvl in range(g + 1):
                            si_ = lvl % 2
                            pg.op(eng, lambda e, src=src, si_=si_, lo=lo, sh=sh, j=j, L=L: e.tensor_tensor(
                                out=Sb[si_][:, j, lo:L], in0=src[:, j, lo:L], in1=src[:, j, lo - sh:L - sh], op=ALU.add),
                                reads=[srcres], writes=[("S", si_, j)])
                            src = Sb[si_]
                            srcres = ("S", si_, j)
                            sh *= 2
                            lo = 2 * lo + 1
                        fin[j] = (src, srcres)
                    src = fin[0][0]
                    srcres_l = [fin[0][1], fin[1][1]]
                    pb = nxt("Pbf", 2)
                    pg.op("dve", lambda e, src=src, ub=ub, pb=pb, wwin=wwin, n=n: e.scalar_tensor_tensor(
                        out=Pbf[pb][:, :, 0:n], in0=src[:, :, 16:16 + n], scalar=1.0 / wwin, in1=Ub[ub][:, :, 16:16 + n],
                        op0=ALU.mult, op1=ALU.subtract), reads=srcres_l + [("U", ub, 0), ("U", ub, 1)], writes=[("Pbf", pb)])
                    if fixf:
                        for jj in range(2):
                            pg.op("pool", lambda e, src=src, g=g, jj=jj: e.tensor_tensor(
                                out=tfix[:, jj, 0:POOLBUF], in0=src[:, jj, 16:16 + POOLBUF],
                                in1=rc[:, g, 0:POOLBUF], op=ALU.mult),
                                reads=srcres_l + [("rc", g)], writes=["tfix"])
                        pg.op("pool", lambda e, ub=ub, pb=pb: e.tensor_tensor(
                            out=Pbf[pb][:, :, 0:POOLBUF], in0=tfix[:, :, 0:POOLBUF], in1=Ub[ub][:, :, 16:16 + POOLBUF], op=ALU.subtract),
                            reads=["tfix", ("U", ub, 0), ("U", ub, 1)], writes=[("Pbf", pb)])
                    if pend[0] is not None:
                        zstage(pend[0])
                    pend[0] = (g, l0, n, pb, sgs)
            zstage(pend[0])
            for j in range(8):
                slot = load_unit(munit_cols(j))
                if mi == 0 and j == 2:
                    load_wout()
                for (t0, l0, n, uh, utag, fixf, iss) in chunk_list:
                    bb = [ALLB[nxt("AB", 7)] for _ in range(4)]
                    gres = [("GAT", fc, t0 // 512) for fc in range(8)]
                    bres_ = [("GBT", fc, l0) for fc in range(8)]
                    for kc in range(KC):
                        mm(psF[bb[0]][:, 0:n], W[slot][:, kc, 0, :], hT[:, kc, t0:t0 + n], kc == 0, kc == KC - 1,
                           reads=[("W", slot)] + hres(t0, n), writes=[("ps", bb[0])])
                    for kc in range(KC):
                        mm(psF[bb[1]][:, 0:n], W[slot][:, kc, 1, :], hT[:, kc, t0:t0 + n], kc == 0, kc == KC - 1,
                           reads=[("W", slot)] + hres(t0, n), writes=[("ps", bb[1])])
                    for kc in range(KC):
                        mm(psF[bb[2]][:, 0:n], W[slot][:, kc, 2, :], GAT[:, kc, t0:t0 + n], kc == 0, kc == KC - 1,
                           reads=[("W", slot)] + gres, writes=[("ps", bb[2])])
                    for kc in range(KC):
                        mm(psF[bb[3]][:, 0:n], W[slot][:, kc, 3, :], GBT[:, kc, l0:l0 + n], kc == 0, kc == KC - 1,
                           reads=[("W", slot)] + bres_, writes=[("ps", bb[3])])
                    r = nxt("sab", 2)
                    pg.op("act", lambda e, r=r, b=bb[0], n=n: e.activation(out=sa[r][:, 0:n], in_=psF[b][:, 0:n], func=AF.Sigmoid),
                          reads=[("ps", bb[0])], writes=[("sa", r)])
                    pg.op("act", lambda e, r=r, b=bb[1], n=n: e.activation(out=sb_[r][:, 0:n], in_=psF[b][:, 0:n], func=AF.Sigmoid),
                          reads=[("ps", bb[1])], writes=[("sb", r)])
                    pg.op("dve", lambda e, r=r, b=bb[2], n=n: e.tensor_tensor(out=sa[r][:, 0:n], in0=psF[b][:, 0:n], in1=sa[r][:, 0:n], op=ALU.mult),
                          reads=[("ps", bb[2]), ("sa", r)], writes=[("sa", r)])
                    pg.op("dve", lambda e, r=r, b=bb[3], n=n: e.tensor_tensor(out=sb_[r][:, 0:n], in0=psF[b][:, 0:n], in1=sb_[r][:, 0:n], op=ALU.mult),
                          reads=[("ps", bb[3]), ("sb", r)], writes=[("sb", r)])
                    pg.op("pool", lambda e, r=r, j=j, l0=l0, n=n: e.tensor_tensor(out=MT[:, j, l0:l0 + n], in0=sa[r][:, 0:n], in1=sb_[r][:, 0:n], op=ALU.add),
                          reads=[("sa", r), ("sb", r)], writes=[("MT", j, l0)])
            tiles = []
            for (t0, l0, n, uh, utag, fixf, iss) in chunk_list:
                if iss:
                    tiles.append((xs[:, :], y_s[:, :], DS, l0, l0))
                else:
                    for ti in range(n // 128):
                        g0 = t0 + ti * 128
                        tiles.append((xp[s, g0:g0 + 128, :], y_p[s, g0:g0 + 128, :], 128, l0 + ti * 128, l0))
            obsel = {}

            def xload(k):
                src_, _, rows, _, _ = tiles[k]
                orr = nxt("o", 3)
                obsel[k] = orr
                pg.op("sp", lambda e, orr=orr, src_=src_, rows=rows: e.dma_start(out=ob[orr][0:rows, :], in_=src_),
                      writes=[("o", orr)], dma="x2%d" % orr)

            xload(0)
            for k in range(len(tiles)):
                if k + 1 < len(tiles):
                    xload(k + 1)
                src_, dst_, rows, lt, l0 = tiles[k]
                orr = obsel[k]
                si = stat_i[0] % 64
                stat_i[0] += 1
                mres = [("MT", fc, l0) for fc in range(8)]
                for half in range(2):
                    b = ALLB[nxt("AB", 7)]
                    for kc in range(KC):
                        mm(psF[b][0:rows, :], MT[:, kc, lt:lt + rows], wout[:, kc, half * 512:(half + 1) * 512], kc == 0, kc == KC - 1,
                           reads=["wout"] + mres, writes=[("ps", b)])
                    pg.op("dve", lambda e, b=b, half=half, orr=orr, rows=rows: e.tensor_tensor(
                        out=ob[orr][0:rows, half * 512:(half + 1) * 512], in0=psF[b][0:rows, :],
                        in1=ob[orr][0:rows, half * 512:(half + 1) * 512], op=ALU.add),
                        reads=[("ps", b), ("o", orr)], writes=[("o", orr)])
                pg.op("act", lambda e, orr=orr, si=si, rows=rows: e.activation(out=Pbf[0][0:rows, :, :].rearrange("p a b -> p (a b)"), in_=ob[orr][0:rows, :], func=AF.Square,
                                                                              accum_out=ssT[0:rows, si:si + 1]),
                      reads=[("o", orr)], writes=[("Pbf", 0), ("ss", si)])
                pg.op("act", lambda e, si=si, rows=rows: e.activation(out=rsT[0:rows, si:si + 1], in_=ssT[0:rows, si:si + 1], func=AF.Ln,
                                                                      scale=1.0 / D, bias=EPS), reads=[("ss", si)], writes=[("rs", si)])
                pg.op("act", lambda e, si=si, rows=rows: e.activation(out=rsT[0:rows, si:si + 1], in_=rsT[0:rows, si:si + 1], func=AF.Exp, scale=-0.5),
                      reads=[("rs", si)], writes=[("rs", si)])
                pg.op("dve", lambda e, orr=orr, si=si, rows=rows: e.scalar_tensor_tensor(
                    out=ob[orr][0:rows, :], in0=ob[orr][0:rows, :], scalar=rsT[0:rows, si:si + 1], in1=fg_bc[0:rows, :],
                    op0=ALU.mult, op1=ALU.mult), reads=[("o", orr), ("rs", si), "fg_bc"], writes=[("o", orr)])
                pg.op("sp", lambda e, orr=orr, dst_=dst_, rows=rows: e.dma_start(out=dst_, in_=ob[orr][0:rows, :]),
                      reads=[("o", orr)], dma="y%d" % orr)
        outs = [(Uhist, "Uhist", pool_p[s])]
        if with_s:
            outs.append((Uhist_s, "UhistS", pool_s))
        for (uh, utag, dst_) in outs:
            for half in range(2):
                b = ALLB[nxt("AB", 7)]
                for q in range(4):
                    fc = half * 4 + q
                    pg.op("pe", lambda e, b=b, q=q, fc=fc, uh=uh: e.transpose(out=psF[b][0:POOLBUF, q * 128:(q + 1) * 128], in_=uh[:, fc, 1:16],
                                                                             identity=ident_f[:]),
                          reads=[(utag, fc // 2, fc % 2), "ident_f"], writes=[("ps", b)])
                pg.op("dve", lambda e, b=b, half=half: e.tensor_copy(out=poolout[0:POOLBUF, half * 512:(half + 1) * 512], in_=psF[b][0:POOLBUF, :]),
                      reads=[("ps", b)], writes=["poolout"])
            pg.op("sp", lambda e, dst_=dst_: e.dma_start(out=dst_, in_=poolout[0:POOLBUF, :]), reads=["poolout"], dma="po")

    issue_unit()
    consts()
    pg.op("pool", lambda e: e.memset(Uhist_s[:], 0.0), writes=[("UhistS", g, j) for g in range(4) for j in range(2)])
    for s in range(NSEQ):
        ws = with_sample and s == NSEQ - 1
        pg.barrier()
        for g in range(4):
            pg.op("pool", lambda e, g=g: e.memset(Uhist[:, 2 * g:2 * g + 2, :], 0.0), writes=[("Uhist", g, 0), ("Uhist", g, 1)])
        stage1_prompt(s, ws)
        if s == 0:
            consts_late()
        pg.barrier()
        stage2(s, ws)
    pg.emit()
    return nc


_NC_CACHE = {}


def kernel(x_prompt, x_sample, cache_k, cache_v, state_pool, norm_g, w_in, w_pool, pool_scale, w_br_a, w_br_b, w_out, final_g):
    n = 8
    f = lambda a: np.ascontiguousarray(np.asarray(a, dtype=np.float32))
    x_prompt, x_sample, cache_k, cache_v, state_pool = map(f, (x_prompt, x_sample, cache_k, cache_v, state_pool))
    B, T, Dm = x_prompt.shape
    DB, DS, _ = x_sample.shape
    PAST = cache_k.shape[2]
    NSEQ = B // n
    key = (NSEQ, T, PAST, DS)
    if key not in _NC_CACHE:
        _NC_CACHE[key] = build_nc(NSEQ=NSEQ, T=T, PAST=PAST, DS=DS)
    nc = _NC_CACHE[key]
    shared = dict(norm_g=f(norm_g[0]), w_in=f(w_in[0]), w_pool=f(w_pool[0]), pool_scale=f(pool_scale[0]),
                  w_br_a=f(w_br_a[0]), w_br_b=f(w_br_b[0]), w_out=f(w_out[0]), final_g=f(final_g))
    in_maps = []
    for c in range(n):
        m = dict(shared)
        m["xp"] = x_prompt[c * NSEQ:(c + 1) * NSEQ]
        m["xs"] = x_sample[c]
        m["ck"] = cache_k[0, c].reshape(PAST, Dm)
        m["cv"] = cache_v[0, c].reshape(PAST, Dm)
        m["spool"] = state_pool[0, c]
        in_maps.append(m)
    res = run_bass_kernel_spmd(nc, in_maps, core_ids=list(range(n)))
    R = res.results
    cat = lambda k: np.concatenate([np.asarray(r[k]) for r in R], axis=0)
    stk = lambda k: np.stack([np.asarray(r[k]) for r in R], axis=0)
    y_prompt = cat("y_p")
    y_sample = stk("y_s")
    k_prompt = cat("k_p").reshape(1, B, T, NH, 128)
    v_prompt = cat("v_p").reshape(1, B, T, NH, 128)
    pool_prompt = cat("pool_p").reshape(1, B, POOLBUF, Dm)
    k_sample = stk("k_s").reshape(1, DB, DS, NH, 128)
    v_sample = stk("v_s").reshape(1, DB, DS, NH, 128)
    pool_sample = stk("pool_s").reshape(1, DB, POOLBUF, Dm)
    return (y_prompt.astype(np.float32), y_sample.astype(np.float32), k_prompt, v_prompt, pool_prompt, k_sample, v_sample, pool_sample)
```
